# Optimizing a Trainium2 kernel written in Bass

```python
import jax, jax.numpy as jnp
from jax import lax
import numpy as np

D_MODEL = 1024
BATCH = 8
SEQ = 8192
DEPTH = 2

MEM_LEN = 256
HEAD_DIM = 64
ROPE_THETA = 10000.0
NORM_EPS = 1e-6
BLOCK = 128

SWA_HEADS = 8
SWA_KV_HEADS = 2
SWA_WINDOW = 128
MLA_HEADS = 8
MLA_Q_RANK = 384
MLA_KV_RANK = 256
MLA_NOPE_DIM = 64
MLA_ROPE_DIM = 32
MLA_V_DIM = 64
A_Q = SWA_HEADS * HEAD_DIM
A_KV = SWA_KV_HEADS * HEAD_DIM
EVEN_IN = A_Q + 2 * A_KV + MLA_Q_RANK + MLA_KV_RANK + MLA_ROPE_DIM
EVEN_SPLITS = [A_Q, A_Q + A_KV, A_Q + 2 * A_KV, A_Q + 2 * A_KV + MLA_Q_RANK,
               A_Q + 2 * A_KV + MLA_Q_RANK + MLA_KV_RANK]
EVEN_OUT = SWA_HEADS * HEAD_DIM + MLA_HEADS * MLA_V_DIM
DIL_HEADS = D_MODEL // HEAD_DIM
DIL_PATTERNS = ((128, 1), (512, 4), (2048, 16))
X_HEADS = 4
X_HEAD_DIM = 128
FFN_HIDDEN = -(-8 * D_MODEL // (3 * 256)) * 256

kernel_name = 'hybrid_swa_mla_dilated_block'


def rms_norm(x, g):
    xf = x.astype(jnp.float32)
    y = xf * lax.rsqrt(jnp.mean(xf * xf, axis=-1, keepdims=True) + NORM_EPS)
    return (y * g.astype(jnp.float32)).astype(x.dtype)


def rope(x, positions):
    dh = x.shape[-1]
    inv_freq = ROPE_THETA ** (-jnp.arange(0, dh, 2, dtype=jnp.float32) / dh)
    ang = positions.astype(jnp.float32)[..., None] * inv_freq
    c = jnp.cos(ang)[:, :, None, :]
    s = jnp.sin(ang)[:, :, None, :]
    x1, x2 = jnp.split(x.astype(jnp.float32), 2, axis=-1)
    return jnp.concatenate([x1 * c - x2 * s, x2 * c + x1 * s], axis=-1).astype(x.dtype)


def banded_attention(q, k, v, max_dist, sink=None):
    b, L, h, dh = q.shape
    g = k.shape[2]
    rep = h // g
    nb = L // BLOCK
    qb = q.reshape(b, nb, BLOCK, g, rep, dh)

    def two_blocks(t):
        tb = t.reshape(b, nb, BLOCK, g, t.shape[-1])
        prev = jnp.pad(tb, ((0, 0), (1, 0), (0, 0), (0, 0), (0, 0)))[:, :-1]
        return jnp.concatenate([prev, tb], axis=2)

    kk = two_blocks(k)
    vv = two_blocks(v)
    s = jnp.einsum('bnqgrd,bnkgd->bngrqk', qb, kk).astype(jnp.float32) * (dh ** -0.5)
    qi = jnp.arange(BLOCK)[:, None]
    kj = jnp.arange(2 * BLOCK)[None, :]
    dist = BLOCK + qi - kj
    band = (dist >= 0) & (dist <= max_dist)
    exists = (jnp.arange(nb)[:, None, None] > 0) | (kj >= BLOCK)[None]
    mask = band[None] & exists
    s = jnp.where(mask[None, :, None, None], s, -jnp.inf)
    m = jnp.max(s, axis=-1, keepdims=True)
    if sink is not None:
        sk = sink.astype(jnp.float32).reshape(g, rep)[None, None, :, :, None, None]
        m = jnp.maximum(m, sk)
    p = jnp.exp(s - m)
    l = jnp.sum(p, axis=-1, keepdims=True)
    if sink is not None:
        l = l + jnp.exp(sk - m)
    o = jnp.einsum('bngrqk,bnkgd->bnqgrd', (p / l).astype(v.dtype), vv)
    lse = (m + jnp.log(l))[..., 0].transpose(0, 1, 4, 2, 3).reshape(b, L, h)
    return o.reshape(b, L, h, -1), lse


def causal_mla_attention(q_nope, q_rope, k_nope, k_rope, v):
    S = q_nope.shape[1]
    scale = (q_nope.shape[-1] + q_rope.shape[-1]) ** -0.5
    outs = []
    for i in range(S // BLOCK):
        q0, q1 = i * BLOCK, (i + 1) * BLOCK
        s = (jnp.einsum('bqhd,bkhd->bhqk', q_nope[:, q0:q1], k_nope[:, :q1])
             + jnp.einsum('bqhd,bkd->bhqk', q_rope[:, q0:q1], k_rope[:, :q1])).astype(jnp.float32) * scale
        causal = jnp.arange(q0, q1)[:, None] >= jnp.arange(q1)[None, :]
        p = jax.nn.softmax(jnp.where(causal, s, -jnp.inf), axis=-1).astype(v.dtype)
        outs.append(jnp.einsum('bhqk,bkhd->bqhd', p, v[:, :q1]))
    return jnp.concatenate(outs, axis=1)


def even_mixer(h, positions, w_in, sinks, q_norm, w_uq, kv_norm, w_ukv, w_out):
    b, s, _ = h.shape
    z = h @ w_in
    qa, ka, va, cq, ckv, kr = jnp.split(z, EVEN_SPLITS, axis=-1)
    qa = rope(qa.reshape(b, s, SWA_HEADS, HEAD_DIM), positions)
    ka = rope(ka.reshape(b, s, SWA_KV_HEADS, HEAD_DIM), positions)
    va = va.reshape(b, s, SWA_KV_HEADS, HEAD_DIM)
    oa, _ = banded_attention(qa, ka, va, SWA_WINDOW - 1, sink=sinks)
    qb = (rms_norm(cq, q_norm) @ w_uq).reshape(b, s, MLA_HEADS, MLA_NOPE_DIM + MLA_ROPE_DIM)
    q_nope, q_rope = jnp.split(qb, [MLA_NOPE_DIM], axis=-1)
    q_rope = rope(q_rope, positions)
    kvb = (rms_norm(ckv, kv_norm) @ w_ukv).reshape(b, s, MLA_HEADS, MLA_NOPE_DIM + MLA_V_DIM)
    k_nope, vb = jnp.split(kvb, [MLA_NOPE_DIM], axis=-1)
    k_rope = rope(kr[:, :, None, :], positions)[:, :, 0]
    ob = causal_mla_attention(q_nope, q_rope, k_nope, k_rope, vb)
    o = jnp.concatenate([oa.reshape(b, s, -1), ob.reshape(b, s, -1)], axis=-1)
    return o @ w_out


def dilated_attention(q, k, v):
    b, s, h, dh = q.shape
    outs, lses = [], []
    for window, dil in DIL_PATTERNS:
        span = dil * BLOCK
        L = -(-s // span) * span
        n = L // dil

        def deinterleave(t):
            t = jnp.pad(t, ((0, 0), (0, L - s), (0, 0), (0, 0)))
            return t.reshape(b, n, dil, h, dh).transpose(0, 2, 1, 3, 4).reshape(b * dil, n, h, dh)

        o, lse = banded_attention(deinterleave(q), deinterleave(k), deinterleave(v), window // dil)
        outs.append(o.reshape(b, dil, n, h, dh).transpose(0, 2, 1, 3, 4).reshape(b, L, h, dh)[:, :s])
        lses.append(lse.reshape(b, dil, n, h).transpose(0, 2, 1, 3).reshape(b, L, h)[:, :s])
    wts = jax.nn.softmax(jnp.stack(lses, axis=-1), axis=-1).astype(q.dtype)
    return jnp.einsum('bshdn,bshn->bshd', jnp.stack(outs, axis=-1), wts)


def odd_mixer(h, positions, w_qkv, w_out):
    b, s, _ = h.shape
    q, k, v = jnp.split((h @ w_qkv).reshape(b, s, 3 * DIL_HEADS, HEAD_DIM), 3, axis=2)
    o = dilated_attention(rope(q, positions), rope(k, positions), v)
    return o.reshape(b, s, -1) @ w_out


def memory_cross_attention(h, mem_n, w_q, w_kv, w_o):
    b, s, _ = h.shape
    q = (h @ w_q).reshape(b, s, X_HEADS, X_HEAD_DIM)
    k, v = jnp.split((mem_n @ w_kv).reshape(b, mem_n.shape[1], 2 * X_HEADS, X_HEAD_DIM), 2, axis=2)
    sc = jnp.einsum('bqhd,bkhd->bhqk', q, k).astype(jnp.float32) * (X_HEAD_DIM ** -0.5)
    p = jax.nn.softmax(sc, axis=-1).astype(v.dtype)
    o = jnp.einsum('bhqk,bkhd->bqhd', p, v).reshape(b, s, -1)
    return o @ w_o


def swiglu(h, w_gate, w_up, w_down):
    return (jax.nn.silu(h @ w_gate) * (h @ w_up)) @ w_down


def setup_inputs(seed: int = 0) -> dict:
    key = jax.random.key(seed)
    keys = iter(jax.random.split(key, 64))

    def w(shape, fan_in, gain=1.0):
        return jax.random.normal(next(keys), shape, jnp.float32) * (gain * fan_in ** -0.5)

    def gain_vec(n):
        return 1.0 + 0.02 * jax.random.normal(next(keys), (n,), jnp.float32)

    res_gain = (2.0 * DEPTH) ** -0.5
    inp = {}
    inp['x'] = jax.random.normal(next(keys), (BATCH, SEQ, D_MODEL), jnp.float32)
    inp['mem'] = jax.random.normal(next(keys), (BATCH, MEM_LEN, D_MODEL), jnp.float32)
    offsets = jax.random.randint(next(keys), (BATCH, 1), 0, 4096, dtype=jnp.int32)
    inp['positions'] = jnp.arange(SEQ, dtype=jnp.int32)[None, :] + offsets
    for l in range(DEPTH):
        p = 'l%d_' % l
        inp[p + 'mix_norm'] = gain_vec(D_MODEL)
        if l % 2 == 0:
            inp[p + 'w_in'] = w((D_MODEL, EVEN_IN), D_MODEL)
            inp[p + 'sinks'] = jax.random.normal(next(keys), (SWA_HEADS,), jnp.float32)
            inp[p + 'q_norm'] = gain_vec(MLA_Q_RANK)
            inp[p + 'w_uq'] = w((MLA_Q_RANK, MLA_HEADS * (MLA_NOPE_DIM + MLA_ROPE_DIM)), MLA_Q_RANK)
            inp[p + 'kv_norm'] = gain_vec(MLA_KV_RANK)
            inp[p + 'w_ukv'] = w((MLA_KV_RANK, MLA_HEADS * (MLA_NOPE_DIM + MLA_V_DIM)), MLA_KV_RANK)
            inp[p + 'w_out'] = w((EVEN_OUT, D_MODEL), EVEN_OUT, res_gain)
        else:
            inp[p + 'w_qkv'] = w((D_MODEL, 3 * DIL_HEADS * HEAD_DIM), D_MODEL)
            inp[p + 'w_out'] = w((DIL_HEADS * HEAD_DIM, D_MODEL), DIL_HEADS * HEAD_DIM, res_gain)
        inp[p + 'x_norm'] = gain_vec(D_MODEL)
        inp[p + 'mem_norm'] = gain_vec(D_MODEL)
        inp[p + 'w_xq'] = w((D_MODEL, X_HEADS * X_HEAD_DIM), D_MODEL)
        inp[p + 'w_xkv'] = w((D_MODEL, 2 * X_HEADS * X_HEAD_DIM), D_MODEL)
        inp[p + 'w_xo'] = w((X_HEADS * X_HEAD_DIM, D_MODEL), X_HEADS * X_HEAD_DIM, res_gain)
        inp[p + 'ffn_norm'] = gain_vec(D_MODEL)
        inp[p + 'w_gate'] = w((D_MODEL, FFN_HIDDEN), D_MODEL)
        inp[p + 'w_up'] = w((D_MODEL, FFN_HIDDEN), D_MODEL)
        inp[p + 'w_down'] = w((FFN_HIDDEN, D_MODEL), FFN_HIDDEN, res_gain)
    inp['final_norm'] = gain_vec(D_MODEL)
    return inp


def reference(x, mem, positions,
              l0_mix_norm, l0_w_in, l0_sinks, l0_q_norm, l0_w_uq, l0_kv_norm, l0_w_ukv, l0_w_out,
              l0_x_norm, l0_mem_norm, l0_w_xq, l0_w_xkv, l0_w_xo,
              l0_ffn_norm, l0_w_gate, l0_w_up, l0_w_down,
              l1_mix_norm, l1_w_qkv, l1_w_out,
              l1_x_norm, l1_mem_norm, l1_w_xq, l1_w_xkv, l1_w_xo,
              l1_ffn_norm, l1_w_gate, l1_w_up, l1_w_down,
              final_norm):
    mixers = [
        lambda h: even_mixer(h, positions, l0_w_in, l0_sinks, l0_q_norm, l0_w_uq,
                             l0_kv_norm, l0_w_ukv, l0_w_out),
        lambda h: odd_mixer(h, positions, l1_w_qkv, l1_w_out),
    ]
    mix_norms = [l0_mix_norm, l1_mix_norm]
    xattn = [(l0_x_norm, l0_mem_norm, l0_w_xq, l0_w_xkv, l0_w_xo),
             (l1_x_norm, l1_mem_norm, l1_w_xq, l1_w_xkv, l1_w_xo)]
    ffns = [(l0_ffn_norm, l0_w_gate, l0_w_up, l0_w_down),
            (l1_ffn_norm, l1_w_gate, l1_w_up, l1_w_down)]
    for layer in range(DEPTH):
        x = x + mixers[layer](rms_norm(x, mix_norms[layer]))
        xn, mn, wq, wkv, wo = xattn[layer]
        x = x + memory_cross_attention(rms_norm(x, xn), rms_norm(mem, mn), wq, wkv, wo)
        fn, wg, wu, wd = ffns[layer]
        x = x + swiglu(rms_norm(x, fn), wg, wu, wd)
    return rms_norm(x, final_norm)
```

```python
import math
from contextlib import ExitStack

import numpy as np
import concourse.bass as bass
import concourse.mybir as mybir
from concourse.bass_utils import run_bass_kernel_spmd

F32 = mybir.dt.float32
BF16 = mybir.dt.bfloat16
I32 = mybir.dt.int32
ALU = mybir.AluOpType
AF = mybir.ActivationFunctionType

S = 8192
D = 1024
NCORE = 8
HID = 2816
EPS = 1e-6
SB_BASE = 16512
SB_END = 229376
NDMA = 8
PI = math.pi
CW1 = 6.28125
CW2 = float(np.float32(2 * math.pi - CW1))


class Buf:
    __slots__ = ("name", "w", "r", "psum")

    def __init__(self, name="", psum=False):
        self.name = name
        self.w = None
        self.r = {}
        self.psum = psum


class EngQ:
    def __init__(self, name):
        self.name = name
        self.count = 0
        self.waited = {}
        self.prog = []
        self.dsem_i = 0


class Tile:
    def __init__(self, t, nb, name):
        self.t = t
        self.b = [Buf(name) for _ in range(nb)]

    def __getitem__(self, k):
        return self.t[k]


class Sched:
    def __init__(self, nc, es):
        self.nc = nc
        self.q = {n: EngQ(n) for n in ("pe", "act", "dve", "pool", "sp")}
        self.sems = {}
        for n in self.q:
            self.sems[n] = es.enter_context(nc.semaphore("c_" + n))
        self.dcount = {}
        for qn in ("sp", "pool"):
            for i in range(NDMA):
                k = "d_%s%d" % (qn, i)
                self.sems[k] = es.enter_context(nc.semaphore(k))
                self.dcount[k] = 0
        self.sb_off = SB_BASE
        self.sb_mark = []
        self.nalloc = 0
        self.psum = []
        for i in range(8):
            t = Tile(nc.alloc_psum_tensor("ps%d" % i, [128, 512], F32), 1, "ps%d" % i)
            t.b[0].psum = True
            self.psum.append(t)

    def tile(self, shape, dtype, name="t", nb=1):
        self.nalloc += 1
        esz = 4 if dtype in (F32, I32) else 2
        n = 1
        for x in shape[1:]:
            n *= x
        nbytes = (n * esz + 63) // 64 * 64
        off = self.sb_off
        assert off + nbytes <= SB_END, "SBUF overflow at %s: need %d more" % (name, off + nbytes - SB_END)
        self.sb_off += nbytes
        h = self.nc.alloc_sbuf_tensor_at("%s_%d" % (name, self.nalloc), list(shape), dtype, offset=off)
        return Tile(h, nb, name)

    def push(self):
        self.sb_mark.append(self.sb_off)

    def pop(self):
        self.sb_off = self.sb_mark.pop()

    def _deps(self, E, reads, writes):
        deps = {}

        def add(d, same_ok):
            if d is None:
                return
            k, v = d
            if k == E.name and not same_ok:
                return
            if deps.get(k, 0) < v:
                deps[k] = v
        same = E.name != "pe"
        for b in reads:
            add(b.w, same)
            if b.psum:
                for k, v in b.r.items():
                    add((k, v), False)
        for b in writes:
            add(b.w, same)
            for k, v in b.r.items():
                add((k, v), same)
        waits = []
        for k, v in deps.items():
            if E.waited.get(k, 0) < v:
                E.waited[k] = v
                waits.append((k, v))
        return waits

    def _done(self, reads, writes, done):
        k, v = done
        for b in reads:
            if b.r.get(k, 0) < v:
                b.r[k] = v
        for b in writes:
            b.w = done
            b.r = {}

    def op(self, eng, fn, reads=(), writes=()):
        E = self.q[eng]
        waits = self._deps(E, reads, writes)
        E.count += 1
        E.prog.append((waits, fn, (eng, 1)))
        self._done(reads, writes, (eng, E.count))

    def dma(self, eng, out, in_, reads=(), writes=()):
        E = self.q[eng]
        k = "d_%s%d" % (eng, E.dsem_i % NDMA)
        E.dsem_i += 1
        waits = self._deps(E, reads, writes)
        prev = self.dcount[k]
        if prev > 0 and E.waited.get(k, 0) < prev:
            E.waited[k] = prev
            waits.append((k, prev))
        self.dcount[k] = prev + 16
        E.prog.append((waits, lambda e, o=out, i=in_: e.dma_start(out=o, in_=i), (k, 16)))
        self._done(reads, writes, (k, prev + 16))

    def barrier(self):
        allv = [(n, E.count) for n, E in self.q.items() if E.count]
        allv += [(k, v) for k, v in self.dcount.items() if v]
        for n, E in self.q.items():
            waits = []
            for k, v in allv:
                if k != n and E.waited.get(k, 0) < v:
                    E.waited[k] = v
                    waits.append((k, v))
            if waits:
                E.prog.append((waits, None, None))

    def emit(self):
        nc = self.nc
        sems = self.sems
        self.barrier()
        with nc.Block() as block:
            def run(E, e):
                for waits, fn, inc in E.prog:
                    for k, v in waits:
                        e.wait_ge(sems[k], v)
                    if fn is not None:
                        fn(e).then_inc(sems[inc[0]], inc[1])

            @block.tensor
            def _(e):
                run(self.q["pe"], e)

            @block.scalar
            def _(e):
                run(self.q["act"], e)

            @block.vector
            def _(e):
                run(self.q["dve"], e)

            @block.gpsimd
            def _(e):
                run(self.q["pool"], e)

            @block.sync
            def _(e):
                run(self.q["sp"], e)

    def mm(self, out, lhsT, rhs, start, stop, reads, writes):
        self.op("pe", lambda e: e.matmul(out, lhsT, rhs, start=start, stop=stop), reads, writes)

    def tr(self, out, in_, ident, reads, writes):
        self.op("pe", lambda e: e.transpose(out, in_, ident), reads, writes)

    def act(self, out, in_, func, reads, writes, bias=0.0, scale=1.0):
        self.op("act", lambda e: e.activation(out=out, in_=in_, func=func, bias=bias, scale=scale), reads, writes)

    def cp(self, eng, out, in_, reads, writes):
        if eng == "act":
            self.act(out, in_, AF.Copy, reads, writes)
        else:
            self.op(eng, lambda e: e.tensor_copy(out=out, in_=in_), reads, writes)

    def tt(self, eng, out, in0, in1, op, reads, writes):
        self.op(eng, lambda e: e.tensor_tensor(out=out, in0=in0, in1=in1, op=op), reads, writes)

    def ts(self, eng, out, in0, s1, s2, op0, op1, reads, writes):
        if s2 is None:
            self.op(eng, lambda e: e.tensor_scalar(out=out, in0=in0, scalar1=s1, scalar2=None, op0=op0), reads, writes)
        else:
            self.op(eng, lambda e: e.tensor_scalar(out=out, in0=in0, scalar1=s1, scalar2=s2, op0=op0, op1=op1),
                    reads, writes)

    def stt(self, out, in0, scalar, in1, op0, op1, reads, writes):
        self.op("dve", lambda e: e.scalar_tensor_tensor(out=out, in0=in0, scalar=scalar, in1=in1, op0=op0, op1=op1),
                reads, writes)

    def recip(self, out, in_, reads, writes):
        self.op("dve", lambda e: e.reciprocal(out=out, in_=in_), reads, writes)

    def memset(self, eng, ap, val, writes):
        self.op(eng, lambda e: e.memset(ap, val), (), writes)


class Ctx:
    pass


DBG = {"nt": None, "stop": 99}


def load_weight(s, stg, dram, K, N, name, conv_engs=("act", "dve")):
    KC = (K + 127) // 128
    W = s.tile([128, KC, N], BF16, name)
    CH = 1024
    i = 0
    for kc in range(KC):
        rows = min(128, K - kc * 128)
        for c0 in range(0, N, CH):
            c1 = min(N, c0 + CH)
            st = stg[s.stg_i % len(stg)]
            s.stg_i += 1
            s.dma("sp", st.t[0:rows, 0:c1 - c0], dram[kc * 128:kc * 128 + rows, c0:c1], writes=[st.b[0]])
            eng = conv_engs[i % len(conv_engs)]
            i += 1
            s.cp(eng, W.t[0:rows, kc, c0:c1], st.t[0:rows, 0:c1 - c0], [st.b[0]], [W.b[0]])
    return W


def load_small(s, dram_ap, shape, dtype, name):
    t = s.tile(shape, dtype, name)
    s.dma("sp", t.t[:], dram_ap, writes=[t.b[0]])
    return t


def consts_common(s, C, din):
    C.stg = [s.tile([128, 1024], F32, "stg") for _ in range(2)]
    s.stg_i = 0
    io = s.tile([128, 512], F32, "iota")
    s.op("pool", lambda e: e.iota(io.t[:], pattern=[[1, 512]], base=0, channel_multiplier=-1,
                                  allow_small_or_imprecise_dtypes=True), (), [io.b[0]])
    C.io = io
    C.ident = s.tile([128, 128], BF16, "ident")
    s.ts("dve", C.ident.t[:], io.t[:, 0:128], 0.0, None, ALU.is_equal, None, [io.b[0]], [C.ident.b[0]])
    C.ones = {}
    for n in (1024, 384, 256):
        o = s.tile([128, 128], BF16, "ones%d" % n)
        s.memset("dve", o.t[:], 1.0 / n, [o.b[0]])
        C.ones[n] = o
    o = s.tile([128, 128], BF16, "ones1")
    s.memset("dve", o.t[:], 1.0, [o.b[0]])
    C.ones[1] = o
    for nm in ("perm64", "perm96"):
        st = load_small(s, din[nm][:, :], [128, 128], F32, nm + "f")
        p = s.tile([128, 128], BF16, nm)
        s.cp("dve", p.t[:], st.t[:], [st.b[0]], [p.b[0]])
        setattr(C, nm, p)
    C.invf64 = load_small(s, din["invf64"][:, :], [128, 1], F32, "invf64")
    C.invf96 = load_small(s, din["invf96"][:, :], [128, 1], F32, "invf96")


def rope_tables(s, C, posf, invf, Ct, St, tmp, rows=128):
    ang, kt, ki, rr, mt = tmp
    R = slice(0, rows)
    s.act(ang.t[R], posf.t[R], AF.Copy, [posf.b[0], invf.b[0]], [ang.b[0]], scale=invf.t[R, 0:1])
    s.act(kt.t[R], ang.t[R], AF.Copy, [ang.b[0]], [kt.b[0]], scale=1.0 / (2 * PI))
    s.cp("dve", ki.t[R], kt.t[R], [kt.b[0]], [ki.b[0]])
    s.cp("dve", kt.t[R], ki.t[R], [ki.b[0]], [kt.b[0]])
    s.stt(rr.t[R], kt.t[R], -CW1, ang.t[R], ALU.mult, ALU.add, [kt.b[0], ang.b[0]], [rr.b[0]])
    s.stt(rr.t[R], kt.t[R], -CW2, rr.t[R], ALU.mult, ALU.add, [kt.b[0], rr.b[0]], [rr.b[0]])
    s.ts("dve", ang.t[R], rr.t[R], -3.1415925, 3.1415925, ALU.max, ALU.min, [rr.b[0]], [ang.b[0]])
    s.act(St.t[R], ang.t[R], AF.Sin, [ang.b[0]], [St.b[0]])
    s.ts("dve", mt.t[R], rr.t[R], PI / 2, PI, ALU.add, ALU.is_gt, [rr.b[0]], [mt.b[0]])
    s.stt(kt.t[R], mt.t[R], -2 * PI, rr.t[R], ALU.mult, ALU.add, [mt.b[0], rr.b[0]], [kt.b[0]])
    s.act(Ct.t[R], kt.t[R], AF.Sin, [kt.b[0]], [Ct.b[0]], bias=PI / 2)


class PsView:
    def __init__(self, bank, half):
        self.bank = bank
        self.half = half
        self.b = [Buf("psh", psum=True)]

    @property
    def t(self):
        return self.bank.t[:, self.half * 256:(self.half + 1) * 256]


class PsPool:
    def __init__(self, s, idx, halves=False):
        if halves:
            self.tiles = [PsView(s.psum[i], h) for i in idx for h in range(2)]
        else:
            self.tiles = [s.psum[i] for i in idx]
        self.i = 0

    def get(self):
        t = self.tiles[self.i % len(self.tiles)]
        self.i += 1
        return t


def rms_stat(s, C, n_feat, sq, kcs, rstd, psp, ncol, lnexp=False):
    ps = psp.get()
    for i, kc in enumerate(kcs):
        s.mm(ps.t[:, 0:ncol], C.ones[n_feat].t[:], sq.t[:, kc, 0:ncol], i == 0, i == len(kcs) - 1,
             [C.ones[n_feat].b[0], sq.b[0]], [ps.b[0]])
    if lnexp:
        s.act(rstd.t[:, 0:ncol], ps.t[:, 0:ncol], AF.Ln, [ps.b[0], C.eps.b[0]], [rstd.b[0]], bias=C.eps.t[:, 0:1])
        s.act(rstd.t[:, 0:ncol], rstd.t[:, 0:ncol], AF.Exp, [rstd.b[0]], [rstd.b[0]], scale=-0.5)
        return
    s.act(rstd.t[:, 0:ncol], ps.t[:, 0:ncol], AF.Sqrt, [ps.b[0], C.eps.b[0]], [rstd.b[0]], bias=C.eps.t[:, 0:1])
    s.recip(rstd.t[:, 0:ncol], rstd.t[:, 0:ncol], [rstd.b[0]], [rstd.b[0]])


def rope_chunk(s, C, ps, rows, perm, Ct, St, zb, t1, t2, out_ap, out_bufs, psp, pend):
    R = slice(0, rows)
    s.cp("act", zb.t[R], ps.t[R], [ps.b[0]], [zb.b[0]])

    def rest():
        pr = psp.get()
        s.mm(pr.t[R], perm.t[R, 0:rows], zb.t[R], True, True, [perm.b[0], zb.b[0]], [pr.b[0]])
        s.tt("dve", t1.t[R], ps.t[R], Ct.t[R], ALU.mult, [ps.b[0], Ct.b[0]], [t1.b[0]])
        s.tt("dve", t2.t[R], pr.t[R], St.t[R], ALU.mult, [pr.b[0], St.b[0]], [t2.b[0]])
        s.tt("dve", out_ap, t1.t[R], t2.t[R], ALU.add, [t1.b[0], t2.b[0]], out_bufs)
    pend.append(rest)


def flush(pend):
    while pend:
        pend.pop(0)()


def phase_A0(s, C, din, dsc, x_src):
    s.push()
    TT = 512
    w_in = load_weight(s, C.stg, din["l0_w_in"], 1024, 1632, "w_in")
    w_uq = load_weight(s, C.stg, din["l0_w_uq"], 384, 768, "w_uq")
    w_ukv = load_weight(s, C.stg, din["l0_w_ukv"], 256, 1024, "w_ukv")
    g_mix = load_small(s, din["l0_mix_norm"][:, :], [128, 8], F32, "g_mix")
    g_q = load_small(s, din["l0_q_norm"][:, :], [128, 3], F32, "g_q")
    g_kv = load_small(s, din["l0_kv_norm"][:, :], [128, 2], F32, "g_kv")
    xt = s.tile([128, 8, TT], F32, "xt")
    sq = s.tile([128, 8, TT], BF16, "sq")
    xns = [s.tile([128, 8, TT], BF16, "xn") for _ in range(2)]
    rstds = [s.tile([128, TT], F32, "rstd") for _ in range(2)]
    posi = s.tile([128, TT], I32, "posi")
    posf = s.tile([128, TT], F32, "posf")
    tmp = [s.tile([128, TT], F32, "rt%d" % i) for i in range(3)]
    tmp = [tmp[0], tmp[1], s.tile([128, TT], I32, "rki"), tmp[2], s.tile([128, TT], F32, "rmt")]
    tmpb = tmp
    C64 = s.tile([128, TT], F32, "C64")
    S64 = s.tile([128, TT], F32, "S64")
    C96 = s.tile([128, TT], F32, "C96")
    S96 = s.tile([128, TT], F32, "S96")
    zb = [s.tile([128, TT], BF16, "zb") for _ in range(2)]
    t1 = [s.tile([128, TT], F32, "t1") for _ in range(2)]
    t2 = [s.tile([128, TT], F32, "t2") for _ in range(2)]
    cq = s.tile([128, 3, TT], F32, "cq")
    cqs = s.tile([128, 3, TT], BF16, "cqs")
    cqn = s.tile([128, 3, TT], BF16, "cqn")
    ckv = s.tile([128, 2, TT], F32, "ckv")
    ckvs = s.tile([128, 2, TT], BF16, "ckvs")
    ckvn = s.tile([128, 2, TT], BF16, "ckvn")
    rq = s.tile([128, TT], F32, "rq")
    rkv = s.tile([128, TT], F32, "rkv")
    krr = s.tile([128, TT], BF16, "krr")
    NB = 2
    o_qa = [s.tile([128, 4, TT], BF16, "o_qa") for _ in range(NB)]
    o_ka = [s.tile([128, 2, TT], BF16, "o_ka") for _ in range(NB)]
    o_va = [s.tile([128, TT], BF16, "o_va") for _ in range(NB)]
    o_qm = [s.tile([128, 8, TT], BF16, "o_qm") for _ in range(NB)]
    o_km = [s.tile([128, 8, TT], BF16, "o_km") for _ in range(NB)]
    o_vm = [s.tile([128, 4, TT], BF16, "o_vm") for _ in range(NB)]
    psp = PsPool(s, range(8))
    xsrc = x_src.rearrange("(c p) t -> p c t", p=128)
    ri = 0
    pend = []
    NTL = DBG["nt"] or S // TT

    def front_a(it):
        s.dma("sp", xt.t[:], xsrc[:, :, it * TT:(it + 1) * TT], writes=[xt.b[0]])
        s.act(sq.t[:], xt.t[:], AF.Square, [xt.b[0]], [sq.b[0]])

    def front_b(it):
        xn_, rstd_ = xns[it % 2], rstds[it % 2]
        rms_stat(s, C, 1024, sq, range(8), rstd_, psp, TT)
        for c in range(8):
            s.stt(xn_.t[:, c, :], xt.t[:, c, :], g_mix.t[:, c:c + 1], rstd_.t[:], ALU.mult, ALU.mult,
                  [xt.b[0], g_mix.b[0], rstd_.b[0]], [xn_.b[0]])

    front_a(0)
    front_b(0)
    for it in range(NTL):
        t0 = it * TT
        bi = it % NB
        xn = xns[it % 2]
        s.dma("sp", posi.t[:], din["pos"][0:1, t0:t0 + TT].partition_broadcast(128), writes=[posi.b[0]])
        s.cp("dve", posf.t[:], posi.t[:], [posi.b[0]], [posf.b[0]])
        rope_tables(s, C, posf, C.invf64, C64, S64, tmp)
        rope_tables(s, C, posf, C.invf96, C96, S96, tmpb)

        def proj(W, kcs, col0, M, src, ps):
            for i, kc in enumerate(kcs):
                s.mm(ps.t[0:M, :], W.t[:, kc, col0:col0 + M], src.t[:, kc, :], i == 0, i == len(kcs) - 1,
                     [W.b[0], src.b[0]], [ps.b[0]])
            flush(pend)
        for j in range(6):
            ps = psp.get()
            proj(w_in, range(8), j * 128, 128, xn, ps)
            if j < 4:
                oap, ob = o_qa[bi].t[:, j, :], o_qa[bi].b
            else:
                oap, ob = o_ka[bi].t[:, j - 4, :], o_ka[bi].b
            rope_chunk(s, C, ps, 128, C.perm64, C64, S64, zb[ri % 2], t1[ri % 2], t2[ri % 2], oap, ob, psp, pend)
            ri += 1
            if j == 1 and it + 1 < NTL:
                front_a(it + 1)
        ps = psp.get()
        proj(w_in, range(8), 768, 128, xn, ps)
        s.cp("act", o_va[bi].t[:], ps.t[:], [ps.b[0]], o_va[bi].b)
        if it + 1 < NTL:
            front_b(it + 1)
        for j in range(3):
            ps = psp.get()
            proj(w_in, range(8), 896 + j * 128, 128, xn, ps)
            s.cp("act", cq.t[:, j, :], ps.t[:], [ps.b[0]], [cq.b[0]])
        s.act(cqs.t[:], cq.t[:], AF.Square, [cq.b[0]], [cqs.b[0]])
        rms_stat(s, C, 384, cqs, range(3), rq, psp, TT)
        for c in range(3):
            s.stt(cqn.t[:, c, :], cq.t[:, c, :], g_q.t[:, c:c + 1], rq.t[:], ALU.mult, ALU.mult,
                  [cq.b[0], g_q.b[0], rq.b[0]], [cqn.b[0]])
        for j in range(2):
            ps = psp.get()
            proj(w_in, range(8), 1280 + j * 128, 128, xn, ps)
            s.cp("act", ckv.t[:, j, :], ps.t[:], [ps.b[0]], [ckv.b[0]])
        s.act(ckvs.t[:], ckv.t[:], AF.Square, [ckv.b[0]], [ckvs.b[0]])
        rms_stat(s, C, 256, ckvs, range(2), rkv, psp, TT)
        for c in range(2):
            s.stt(ckvn.t[:, c, :], ckv.t[:, c, :], g_kv.t[:, c:c + 1], rkv.t[:], ALU.mult, ALU.mult,
                  [ckv.b[0], g_kv.b[0], rkv.b[0]], [ckvn.b[0]])
        ps = psp.get()
        proj(w_in, range(8), 1536, 96, xn, ps)
        rope_chunk(s, C, ps, 96, C.perm96, C96, S96, zb[ri % 2], t1[ri % 2], t2[ri % 2], krr.t[0:96, :], [krr.b[0]],
                   psp, pend)
        ri += 1
        for h in range(8):
            ps = psp.get()
            proj(w_uq, range(3), h * 96, 96, cqn, ps)
            rope_chunk(s, C, ps, 96, C.perm96, C96, S96, zb[ri % 2], t1[ri % 2], t2[ri % 2],
                       o_qm[bi].t[0:96, h, :], o_qm[bi].b, psp, pend)
            ri += 1
        for h in range(8):
            ps = psp.get()
            proj(w_ukv, range(2), h * 64, 64, ckvn, ps)
            s.cp("act", o_km[bi].t[0:64, h, :], ps.t[0:64, :], [ps.b[0]], o_km[bi].b)
            s.cp("act", o_km[bi].t[64:96, h, :], krr.t[64:96, :], [krr.b[0]], o_km[bi].b)
        for j in range(4):
            ps = psp.get()
            proj(w_ukv, range(2), 512 + j * 128, 128, ckvn, ps)
            s.cp("act", o_vm[bi].t[:, j, :], ps.t[:], [ps.b[0]], o_vm[bi].b)
        flush(pend)
        sl = slice(t0, t0 + TT)
        s.dma("pool", dsc["QA"].rearrange("c p t -> p c t")[:, :, sl], o_qa[bi].t[:], reads=o_qa[bi].b)
        s.dma("pool", dsc["KA"].rearrange("c p t -> p c t")[:, :, sl], o_ka[bi].t[:], reads=o_ka[bi].b)
        s.dma("pool", dsc["VA"][:, sl], o_va[bi].t[:], reads=o_va[bi].b)
        s.dma("pool", dsc["QM"].rearrange("c p t -> p c t")[:, :, sl], o_qm[bi].t[0:96], reads=o_qm[bi].b)
        s.dma("pool", dsc["KM"].rearrange("c p t -> p c t")[:, :, sl], o_km[bi].t[0:96], reads=o_km[bi].b)
        s.dma("pool", dsc["VM"].rearrange("c p t -> p c t")[:, :, sl], o_vm[bi].t[:], reads=o_vm[bi].b)
    s.pop()


def phase_A1(s, C, din, dsc, x_src):
    s.push()
    TT = 512
    w = load_weight(s, C.stg, din["l1_w_qkv"], 1024, 3072, "w_qkv")
    g_mix = load_small(s, din["l1_mix_norm"][:, :], [128, 8], F32, "g_mix")
    xt = s.tile([128, 8, TT], F32, "xt")
    sq = s.tile([128, 8, TT], BF16, "sq")
    xns = [s.tile([128, 8, TT], BF16, "xn") for _ in range(2)]
    rstds = [s.tile([128, TT], F32, "rstd") for _ in range(2)]
    posi = s.tile([128, TT], I32, "posi")
    posf = s.tile([128, TT], F32, "posf")
    tmp = [s.tile([128, TT], F32, "rt%d" % i) for i in range(3)]
    tmp = [tmp[0], tmp[1], s.tile([128, TT], I32, "rki"), tmp[2], s.tile([128, TT], F32, "rmt")]
    C64 = s.tile([128, TT], F32, "C64")
    S64 = s.tile([128, TT], F32, "S64")
    zb = [s.tile([128, TT], BF16, "zb") for _ in range(2)]
    t1 = [s.tile([128, TT], F32, "t1") for _ in range(2)]
    t2 = [s.tile([128, TT], F32, "t2") for _ in range(2)]
    NB = 2
    o_q = [s.tile([128, 8, TT], BF16, "o_q") for _ in range(NB)]
    o_k = [s.tile([128, 8, TT], BF16, "o_k") for _ in range(NB)]
    o_v = [s.tile([128, 8, TT], BF16, "o_v") for _ in range(NB)]
    psp = PsPool(s, range(8))
    xsrc = x_src.rearrange("(c p) t -> p c t", p=128)
    ri = 0
    pend = []
    NTL = S // TT

    def front_a(it):
        s.dma("sp", xt.t[:], xsrc[:, :, it * TT:(it + 1) * TT], writes=[xt.b[0]])
        s.act(sq.t[:], xt.t[:], AF.Square, [xt.b[0]], [sq.b[0]])

    def front_b(it):
        xn_, rstd_ = xns[it % 2], rstds[it % 2]
        rms_stat(s, C, 1024, sq, range(8), rstd_, psp, TT)
        for c in range(8):
            s.stt(xn_.t[:, c, :], xt.t[:, c, :], g_mix.t[:, c:c + 1], rstd_.t[:], ALU.mult, ALU.mult,
                  [xt.b[0], g_mix.b[0], rstd_.b[0]], [xn_.b[0]])

    front_a(0)
    front_b(0)
    for it in range(NTL):
        t0 = it * TT
        bi = it % NB
        xn = xns[it % 2]
        s.dma("sp", posi.t[:], din["pos"][0:1, t0:t0 + TT].partition_broadcast(128), writes=[posi.b[0]])
        s.cp("dve", posf.t[:], posi.t[:], [posi.b[0]], [posf.b[0]])
        rope_tables(s, C, posf, C.invf64, C64, S64, tmp)
        for j in range(24):
            if j == 2 and it + 1 < NTL:
                front_a(it + 1)
            if j == 8 and it + 1 < NTL:
                front_b(it + 1)
            ps = psp.get()
            for i in range(8):
                s.mm(ps.t[:], w.t[:, i, j * 128:(j + 1) * 128], xn.t[:, i, :], i == 0, i == 7,
                     [w.b[0], xn.b[0]], [ps.b[0]])
            flush(pend)
            if j < 16:
                o = o_q[bi] if j < 8 else o_k[bi]
                rope_chunk(s, C, ps, 128, C.perm64, C64, S64, zb[ri % 2], t1[ri % 2], t2[ri % 2],
                           o.t[:, j % 8, :], o.b, psp, pend)
                ri += 1
            else:
                s.cp("act", o_v[bi].t[:, j - 16, :], ps.t[:], [ps.b[0]], o_v[bi].b)
        flush(pend)
        sl = slice(t0, t0 + TT)
        s.dma("pool", dsc["Q1"].rearrange("c p t -> p c t")[:, :, sl], o_q[bi].t[:], reads=o_q[bi].b)
        s.dma("pool", dsc["K1"].rearrange("c p t -> p c t")[:, :, sl], o_k[bi].t[:], reads=o_k[bi].b)
        s.dma("pool", dsc["V1"].rearrange("c p t -> p c t")[:, :, sl], o_v[bi].t[:], reads=o_v[bi].b)
    s.pop()


def make_band_masks(s, C):
    C.mask = {}
    for nm, op2 in (("dil", ALU.is_le), ("swa", ALU.is_lt)):
        mf = s.tile([128, 256], F32, "mf" + nm)
        s.ts("dve", mf.t[:, 0:128], C.io.t[:, 0:128], 0.0, None, ALU.is_ge, None, [C.io.b[0]], [mf.b[0]])
        s.ts("dve", mf.t[:, 128:256], C.io.t[:, 0:128], 0.0, None, op2, None, [C.io.b[0]], [mf.b[0]])
        m = s.tile([128, 256], BF16, "mask" + nm)
        s.cp("dve", m.t[:], mf.t[:], [mf.b[0]], [m.b[0]])
        C.mask[nm] = m


def phase_band(s, C, din, dsc, layer):
    s.push()
    make_band_masks(s, C)
    if layer == 0:
        nchunk, patterns, mask, scale = 4, (1,), C.mask["swa"], 64 ** -0.5
        sinks = load_small(s, din["l0_sinks"][:, :], [128, 8], F32, "sinks")
        es = s.tile([128, 8], F32, "es")
        s.act(es.t[:], sinks.t[:], AF.Exp, [sinks.b[0]], [es.b[0]])
    else:
        nchunk, patterns, mask, scale = 8, (1, 4, 16), C.mask["dil"], 64 ** -0.5
        es = None
    Q = s.tile([128, S], BF16, "Q")
    K = s.tile([128, S], BF16, "K")
    VT = s.tile([128, S], BF16, "VT")
    ACC = [s.tile([128, S], F32, "ACC") for _ in range(2)]
    VA = [s.tile([128, 64, 128], BF16, "Vaug") for _ in range(2)]
    for v in VA:
        s.memset("pool", v.t[:, :, 64:128], 1.0, v.b)
    OST = s.tile([128, S], BF16, "OST")
    NP = 10
    pT = [s.tile([128, 256], BF16, "pT") for _ in range(NP)]
    rd = s.tile([128, 2048], F32, "rd")
    rd0 = s.tile([128, 2048], F32, "rd0")
    psS = PsPool(s, (0, 1, 2, 3))
    psO = PsPool(s, (4, 5))
    psT = PsPool(s, (6, 7))
    pi = 0
    for c in range(nchunk):
        if layer == 0:
            g = c // 2
            s.dma("sp", Q.t[:], dsc["QA"][c], writes=[Q.b[0]])
            if c % 2 == 0:
                s.dma("sp", K.t[:], dsc["KA"][g], writes=[K.b[0]])
                if c == 0:
                    s.dma("sp", VT.t[:], dsc["VA"][:, :], writes=[VT.b[0]])
        else:
            s.dma("sp", Q.t[:], dsc["Q1"][c], writes=[Q.b[0]])
            s.dma("sp", K.t[:], dsc["K1"][c], writes=[K.b[0]])
            s.dma("sp", VT.t[:], dsc["V1"][c], writes=[VT.b[0]])
        for dil in patterns:
            nblk = 64 // dil
            Qv = Q.t[:].rearrange("p (i r) -> p r i", r=dil)
            Kv = K.t[:].rearrange("p (i r) -> p r i", r=dil)
            Vv = VT.t[:].rearrange("p (i r) -> p r i", r=dil)
            if not (layer == 0 and c % 2 == 1):
                for b8 in range(8):
                    pt = psT.get()
                    ptb = pt.t[:].bitcast(BF16)
                    for k in range(8):
                        blk = b8 * 8 + k
                        r, n = blk // nblk, blk % nblk
                        s.tr(ptb[:, k * 128:(k + 1) * 128], Vv[:, r, n * 128:(n + 1) * 128], C.ident.t[:],
                             [VT.b[0], C.ident.b[0]], [pt.b[0]])
                    ptv = ptb.rearrange("p (k c) -> p k c", c=128)
                    for hh in range(2):
                        if layer == 0:
                            col = (c // 2) * 64
                        else:
                            col = hh * 64
                        s.cp("act" if hh == 0 else "dve", VA[hh].t[:, b8 * 8:(b8 + 1) * 8, 0:64],
                             ptv[:, :, col:col + 64], [pt.b[0]], VA[hh].b)
            items = [(hh, r, n) for hh in range(2) for r in range(dil) for n in range(nblk)]
            st = {}
            LOOK = 3

            def stage_qk(item):
                nonlocal pi
                hh, r, n = item
                H = slice(hh * 64, hh * 64 + 64)
                nq = 256 if n + 1 < nblk else 128
                ps = psS.get()
                s.mm(ps.t[:, 0:nq], Kv[H, r, n * 128:(n + 1) * 128], Qv[H, r, n * 128:n * 128 + nq],
                     True, True, [K.b[0], Q.b[0]], [ps.b[0]])
                p = pT[pi % NP]
                pi += 1
                s.act(p.t[:, 0:nq], ps.t[:, 0:nq], AF.Exp, [ps.b[0]], [p.b[0]], scale=scale)
                s.tt("dve", p.t[:, 0:nq], p.t[:, 0:nq], mask.t[:, 0:nq], ALU.mult, [p.b[0], mask.b[0]],
                     [p.b[0]])
                st[item] = (p, nq)

            def stage_pv(item):
                hh, r, n = item
                p, nq = st.pop(item)
                ACCv = ACC[hh].t[:].rearrange("p (i r) -> p r i", r=dil)
                blk = r * nblk + n
                if n == 0:
                    st["po", hh] = psO.get()
                po = st["po", hh]
                sl0 = (n % 4) * 128
                s.mm(po.t[:, sl0:sl0 + 128], VA[hh].t[:, blk, :], p.t[:, 0:128], n == 0, True,
                     [VA[hh].b[0], p.b[0]], [po.b[0]])
                if n % 4 == 3 or n == nblk - 1:
                    m0 = (n // 4) * 4
                    wdt = (n - m0 + 1) * 128
                    dst = ACCv[:, r, m0 * 128:m0 * 128 + wdt]
                    if dil == 1:
                        s.cp("dve", dst, po.t[:, 0:wdt], [po.b[0]], ACC[hh].b)
                    else:
                        s.tt("dve", dst, dst, po.t[:, 0:wdt], ALU.add, [po.b[0], ACC[hh].b[0]], ACC[hh].b)
                if nq == 256:
                    if (n + 1) % 4 == 0:
                        st["po", hh] = psO.get()
                        po = st["po", hh]
                    sl1 = ((n + 1) % 4) * 128
                    s.mm(po.t[:, sl1:sl1 + 128], VA[hh].t[:, blk, :], p.t[:, 128:256], True, False,
                         [VA[hh].b[0], p.b[0]], [po.b[0]])

            for i in range(len(items) + LOOK):
                if i < len(items):
                    stage_qk(items[i])
                if i >= LOOK:
                    stage_pv(items[i - LOOK])
        for hh in range(2):
            for c0 in range(0, S, 2048):
                cs = slice(c0, c0 + 2048)
                if es is not None:
                    hcol = 2 * c + hh
                    s.act(rd.t[64:128, :], ACC[hh].t[64:128, cs], AF.Ln, [ACC[hh].b[0], es.b[0]], [rd.b[0]],
                          bias=es.t[64:128, hcol:hcol + 1])
                else:
                    s.act(rd.t[64:128, :], ACC[hh].t[64:128, cs], AF.Ln, [ACC[hh].b[0]], [rd.b[0]])
                s.act(rd0.t[0:64, :], rd.t[64:128, :], AF.Exp, [rd.b[0]], [rd0.b[0]], scale=-1.0)
                s.tt("dve", OST.t[hh * 64:hh * 64 + 64, cs], ACC[hh].t[0:64, cs], rd0.t[0:64, :], ALU.mult,
                     [ACC[hh].b[0], rd0.b[0]], [OST.b[0]])
        s.dma("pool", dsc["OT"][c], OST.t[:], reads=[OST.b[0]])
    s.pop()


def phase_mla(s, C, din, dsc):
    s.push()
    scale = 96 ** -0.5
    mf = s.tile([128, 128], F32, "mf")
    s.ts("dve", mf.t[:], C.io.t[:, 0:128], 0.0, None, ALU.is_ge, None, [C.io.b[0]], [mf.b[0]])
    tri = s.tile([128, 128], BF16, "tri")
    s.ts("dve", tri.t[:], mf.t[:], 30000.0, -30000.0, ALU.mult, ALU.add, [mf.b[0]], [tri.b[0]])
    Q = [s.tile([128, S], BF16, "Qm") for _ in range(2)]
    K = [s.tile([128, S], BF16, "Km") for _ in range(2)]
    VT = s.tile([128, S], BF16, "VTm")
    VA = [s.tile([128, 64, 128], BF16, "Vaug") for _ in range(2)]
    for v in VA:
        s.memset("pool", v.t[:, :, 64:128], 1.0, v.b)
    OST = s.tile([128, S], BF16, "OST")
    NP = 6
    pT = [s.tile([128, 512], BF16, "pT") for _ in range(NP)]
    rd = [s.tile([128, 512], F32, "rd") for _ in range(2)]
    psS = PsPool(s, (0, 1, 2, 3))
    psO = PsPool(s, (4, 5))
    psT = PsPool(s, (6, 7))
    pi = 0
    for h in range(8):
        hh = h % 2
        s.dma("sp", Q[hh].t[0:96, :], dsc["QM"][h], writes=[Q[hh].b[0]])
        s.dma("sp", K[hh].t[0:96, :], dsc["KM"][h], writes=[K[hh].b[0]])
        if hh == 0:
            s.dma("sp", VT.t[:], dsc["VM"][h // 2], writes=[VT.b[0]])
            for b8 in range(8):
                pt = psT.get()
                ptb = pt.t[:].bitcast(BF16)
                for k in range(8):
                    blk = b8 * 8 + k
                    s.tr(ptb[:, k * 128:(k + 1) * 128], VT.t[:, blk * 128:(blk + 1) * 128], C.ident.t[:],
                         [VT.b[0], C.ident.b[0]], [pt.b[0]])
                ptv = ptb.rearrange("p (k c) -> p k c", c=128)
                for h2 in range(2):
                    s.cp("dve", VA[h2].t[:, b8 * 8:(b8 + 1) * 8, 0:64], ptv[:, :, h2 * 64:h2 * 64 + 64],
                         [pt.b[0]], VA[h2].b)
        Qh, Kh, Vh = Q[hh], K[hh], VA[hh]
        for qt in range(16):
            q0 = qt * 512
            nkb = 4 * qt + 4
            po = psO.get()
            pend = []

            def qk(kb):
                j = kb - 4 * qt
                c0 = 128 * j if j > 0 else 0
                ps = psS.get()
                s.mm(ps.t[:, c0:512], Kh.t[0:96, kb * 128:(kb + 1) * 128], Qh.t[0:96, q0 + c0:q0 + 512],
                     True, j < 0, [Kh.b[0], Qh.b[0]], [ps.b[0]])
                if j >= 0:
                    d0 = 128 * j
                    s.mm(ps.t[:, d0:d0 + 128], C.ident.t[:], tri.t[:], False, True, [C.ident.b[0], tri.b[0]],
                         [ps.b[0]])
                return ps, c0, j

            def rest(kb, ps, c0, j):
                nonlocal pi
                p = pT[pi % NP]
                pi += 1
                s.act(p.t[:, c0:512], ps.t[:, c0:512], AF.Exp, [ps.b[0]], [p.b[0]], scale=scale)
                s.mm(po.t[:, c0:512], Vh.t[:, kb, :], p.t[:, c0:512], kb == 0, kb == nkb - 1,
                     [Vh.b[0], p.b[0]], [po.b[0]])
            LOOK = 3
            q_ = []
            for kb in range(min(LOOK, nkb)):
                q_.append((kb,) + qk(kb))
            for kb in range(nkb):
                if kb + LOOK < nkb:
                    q_.append((kb + LOOK,) + qk(kb + LOOK))
                a = q_.pop(0)
                rest(*a)
            r = rd[qt % 2]
            s.recip(r.t[64:128, :], po.t[64:128, :], [po.b[0]], [r.b[0]])
            s.tt("dve", OST.t[hh * 64:hh * 64 + 64, q0:q0 + 512], po.t[0:64, :], r.t[64:128, :], ALU.mult,
                 [po.b[0], r.b[0]], [OST.b[0]])
        if hh == 1:
            s.dma("pool", dsc["OT"][4 + h // 2], OST.t[:], reads=[OST.b[0]])
    s.pop()


def phase_D(s, C, din, dsc, L, x_src):
    s.push()
    TT = 512
    p = "l%d_" % L
    w_out = load_weight(s, C.stg, din[p + "w_out"], 1024, 1024, "w_out")
    w_xq = load_weight(s, C.stg, din[p + "w_xq"], 1024, 512, "w_xq")
    w_xo = load_weight(s, C.stg, din[p + "w_xo"], 512, 1024, "w_xo")
    w_xkv = load_weight(s, C.stg, din[p + "w_xkv"], 1024, 1024, "w_xkv")
    g_x = load_small(s, din[p + "x_norm"][:, :], [128, 8], F32, "g_x")
    g_m = load_small(s, din[p + "mem_norm"][:, :], [128, 8], F32, "g_m")
    psp = PsPool(s, range(8))
    mt = s.tile([128, 8, 256], F32, "memt")
    s.dma("sp", mt.t[:], din["memT"].rearrange("(c p) t -> p c t", p=128), writes=[mt.b[0]])
    msq = s.tile([128, 8, 256], BF16, "msq")
    s.act(msq.t[:], mt.t[:], AF.Square, [mt.b[0]], [msq.b[0]])
    mr = s.tile([128, 256], F32, "mr")
    rms_stat(s, C, 1024, msq, range(8), mr, psp, 256, lnexp=True)
    mn = s.tile([128, 8, 256], BF16, "mn")
    for c in range(8):
        s.stt(mn.t[:, c, :], mt.t[:, c, :], g_m.t[:, c:c + 1], mr.t[:], ALU.mult, ALU.mult,
              [mt.b[0], g_m.b[0], mr.b[0]], [mn.b[0]])
    Kx = s.tile([128, 4, 256], BF16, "Kx")
    for h in range(4):
        ps = psp.get()
        for i in range(8):
            s.mm(ps.t[:, 0:256], w_xkv.t[:, i, h * 128:(h + 1) * 128], mn.t[:, i, :], i == 0, i == 7,
                 [w_xkv.b[0], mn.b[0]], [ps.b[0]])
        s.cp("act", Kx.t[:, h, :], ps.t[:, 0:256], [ps.b[0]], [Kx.b[0]])
    Vx = s.tile([128, 2, 512], BF16, "Vx")
    for b in range(2):
        ps = psp.get()
        for i in range(8):
            s.mm(ps.t[:], mn.t[:, i, b * 128:(b + 1) * 128], w_xkv.t[:, i, 512:1024], i == 0, i == 7,
                 [w_xkv.b[0], mn.b[0]], [ps.b[0]])
        s.cp("act", Vx.t[:, b, :], ps.t[:], [ps.b[0]], [Vx.b[0]])
    NB = 2
    ot = [s.tile([128, 8, TT], BF16, "ot") for _ in range(NB)]
    xt = [s.tile([128, 8, TT], F32, "xt") for _ in range(NB)]
    sqs = [s.tile([128, 8, TT], BF16, "sq") for _ in range(NB)]
    xn = s.tile([128, 8, TT], BF16, "xn")
    rstd = s.tile([128, TT], F32, "rstd")
    qx = s.tile([128, 4, TT], BF16, "qx")
    pT = s.tile([128, 8, TT], BF16, "pT", nb=8)
    at = s.tile([128, 4, TT], BF16, "at", nb=4)
    rden = [s.tile([128, TT], F32, "rden") for _ in range(2)]
    xsrc = x_src.rearrange("(c p) t -> p c t", p=128)
    rdst = dsc["R"].rearrange("(c p) t -> p c t", p=128)
    osrc = dsc["OT"].rearrange("c p t -> p c t")
    scale = 128 ** -0.5
    NTL = S // TT

    def stage_a_pieces(it):
        sl = slice(it * TT, (it + 1) * TT)
        o, x = ot[it % NB], xt[it % NB]

        def loads():
            s.dma("sp", o.t[:], osrc[:, :, sl], writes=[o.b[0]])
            s.dma("sp", x.t[:], xsrc[:, :, sl], writes=[x.b[0]])

        def chunk(m):
            ps = psp.get()
            for i in range(8):
                s.mm(ps.t[:], w_out.t[:, i, m * 128:(m + 1) * 128], o.t[:, i, :], i == 0, i == 7,
                     [w_out.b[0], o.b[0]], [ps.b[0]])
            s.tt("dve", x.t[:, m, :], x.t[:, m, :], ps.t[:], ALU.add, [ps.b[0], x.b[0]], [x.b[0]])

        def fin():
            s.act(sqs[it % NB].t[:], x.t[:], AF.Square, [x.b[0]], [sqs[it % NB].b[0]])
        pcs = [loads]
        for m in range(0, 8, 2):
            pcs.append(lambda m=m: (chunk(m), chunk(m + 1)))
        pcs.append(fin)
        return pcs

    def stage_a(it):
        for p_ in stage_a_pieces(it):
            p_()

    def stage_b(it, nxt):
        def gap():
            if nxt:
                nxt.pop(0)()
        sl = slice(it * TT, (it + 1) * TT)
        x = xt[it % NB]
        gap()
        rms_stat(s, C, 1024, sqs[it % NB], range(8), rstd, psp, TT, lnexp=True)
        gap()
        for c in range(8):
            s.stt(xn.t[:, c, :], x.t[:, c, :], g_x.t[:, c:c + 1], rstd.t[:], ALU.mult, ALU.mult,
                  [x.b[0], g_x.b[0], rstd.b[0]], [xn.b[0]])
        for h in range(4):
            ps = psp.get()
            for i in range(8):
                s.mm(ps.t[:], w_xq.t[:, i, h * 128:(h + 1) * 128], xn.t[:, i, :], i == 0, i == 7,
                     [w_xq.b[0], xn.b[0]], [ps.b[0]])
            s.cp("act", qx.t[:, h, :], ps.t[:], [ps.b[0]], [qx.b[0]])
        gap()
        for h in range(4):
            for b in range(2):
                ps = psp.get()
                s.mm(ps.t[:], Kx.t[:, h, b * 128:(b + 1) * 128], qx.t[:, h, :], True, True, [Kx.b[0], qx.b[0]],
                     [ps.b[0]])
                s.act(pT.t[:, h * 2 + b, :], ps.t[:], AF.Exp, [ps.b[0]], [pT.b[h * 2 + b]], scale=scale)
        gap()
        for h in range(4):
            po = psp.get()
            pd = psp.get()
            for b in range(2):
                s.mm(po.t[:], Vx.t[:, b, h * 128:(h + 1) * 128], pT.t[:, h * 2 + b, :], b == 0, b == 1,
                     [Vx.b[0], pT.b[h * 2 + b]], [po.b[0]])
            for b in range(2):
                s.mm(pd.t[:], C.ones[1].t[:], pT.t[:, h * 2 + b, :], b == 0, b == 1,
                     [C.ones[1].b[0], pT.b[h * 2 + b]], [pd.b[0]])
            r = rden[h % 2]
            s.act(r.t[:], pd.t[:], AF.Ln, [pd.b[0]], [r.b[0]])
            s.act(r.t[:], r.t[:], AF.Exp, [r.b[0]], [r.b[0]], scale=-1.0)
            s.tt("dve", at.t[:, h, :], po.t[:], r.t[:], ALU.mult, [po.b[0], r.b[0]], [at.b[h]])
        gap()
        for m in range(8):
            ps = psp.get()
            for i in range(4):
                s.mm(ps.t[:], w_xo.t[:, i, m * 128:(m + 1) * 128], at.t[:, i, :], i == 0, i == 3,
                     [w_xo.b[0], at.b[i]], [ps.b[0]])
            s.tt("dve", x.t[:, m, :], x.t[:, m, :], ps.t[:], ALU.add, [ps.b[0], x.b[0]], [x.b[0]])
        s.dma("pool", rdst[:, :, sl], x.t[:], reads=[x.b[0]])
        while nxt:
            nxt.pop(0)()

    stage_a(0)
    for it in range(NTL):
        stage_b(it, stage_a_pieces(it + 1) if it + 1 < NTL else [])
    s.pop()


def phase_E(s, C, din, dsc, L, final, y_dst):
    s.push()
    TT = 256
    NTL = S // TT
    p = "l%d_" % L
    wg = load_weight(s, C.stg, din[p + "w_gate"], 1024, HID, "wg")
    wu = load_weight(s, C.stg, din[p + "w_up"], 1024, HID, "wu")
    g_f = load_small(s, din[p + "ffn_norm"][:, :], [128, 8], F32, "g_f")
    if final:
        g_fin = load_small(s, din["final_norm"][:, :], [128, 8], F32, "g_fin")
    xt = [s.tile([128, 8, TT], F32, "xt") for _ in range(2)]
    sq = s.tile([128, 8, TT], BF16, "sq")
    rstd = s.tile([128, TT], F32, "rstd")
    xn = s.tile([128, 8, TT], BF16, "xn")
    act = s.tile([128, 22, TT], BF16, "act", nb=22)
    sg = [s.tile([128, TT], BF16, "sg") for _ in range(2)]
    xo = [s.tile([128, 8, TT], F32, "xo") for _ in range(2)]
    if final:
        sqf = s.tile([128, 8, TT], BF16, "sqf")
        rf = s.tile([128, TT], F32, "rf")
    psG = PsPool(s, (0, 1))
    psU = PsPool(s, (2, 3))
    psD = PsPool(s, (4, 5))
    psN = PsPool(s, (6, 7))
    rsrc = dsc["R"].rearrange("(c p) t -> p c t", p=128)
    ydst = y_dst.rearrange("(c p) t -> p c t", p=128) if final else None

    def load(it):
        s.dma("sp", xt[it % 2].t[:], rsrc[:, :, it * TT:(it + 1) * TT], writes=[xt[it % 2].b[0]])

    def norm_pre(it):
        x = xt[it % 2]
        s.act(sq.t[:], x.t[:], AF.Square, [x.b[0]], [sq.b[0]])

    def norm_stat(it):
        rms_stat(s, C, 1024, sq, range(8), rstd, psN, TT)

    def norm_post(it):
        x = xt[it % 2]
        for c in range(8):
            s.stt(xn.t[:, c, :], x.t[:, c, :], g_f.t[:, c:c + 1], rstd.t[:], ALU.mult, ALU.mult,
                  [x.b[0], g_f.b[0], rstd.b[0]], [xn.b[0]])

    def fin_stat(it):
        o = xo[it % 2]
        s.act(sqf.t[:], o.t[:], AF.Square, [o.b[0]], [sqf.b[0]])
        rms_stat(s, C, 1024, sqf, range(8), rf, psN, TT)
        for c in range(8):
            s.stt(o.t[:, c, :], o.t[:, c, :], g_fin.t[:, c:c + 1], rf.t[:], ALU.mult, ALU.mult,
                  [o.b[0], g_fin.b[0], rf.b[0]], [o.b[0]])
        s.dma("pool", ydst[:, :, it * TT:(it + 1) * TT], o.t[:], reads=[o.b[0]])

    load(0)
    norm_pre(0)
    norm_stat(0)
    norm_post(0)
    wd = load_weight(s, C.stg, din[p + "w_down"], HID, 1024, "wd")
    for it in range(NTL):
        x = xt[it % 2]
        if it + 1 < NTL:
            load(it + 1)
        for j in range(22):
            pg = psG.get()
            pu = psU.get()
            for i in range(8):
                s.mm(pg.t[:, 0:TT], wg.t[:, i, j * 128:(j + 1) * 128], xn.t[:, i, :], i == 0, i == 7,
                     [wg.b[0], xn.b[0]], [pg.b[0]])
            for i in range(8):
                s.mm(pu.t[:, 0:TT], wu.t[:, i, j * 128:(j + 1) * 128], xn.t[:, i, :], i == 0, i == 7,
                     [wu.b[0], xn.b[0]], [pu.b[0]])
            g = sg[j % 2]
            s.act(g.t[:], pg.t[:, 0:TT], AF.Silu, [pg.b[0]], [g.b[0]])
            s.tt("dve", act.t[:, j, :], pu.t[:, 0:TT], g.t[:], ALU.mult, [pu.b[0], g.b[0]], [act.b[j]])
            if j == 2 and it + 1 < NTL:
                norm_pre(it + 1)
        if it + 1 < NTL:
            norm_stat(it + 1)
        if final and it > 0:
            fin_stat(it - 1)
        o = xo[it % 2]
        for m in range(8):
            pd = psD.get()
            for j in range(22):
                s.mm(pd.t[:, 0:TT], wd.t[:, j, m * 128:(m + 1) * 128], act.t[:, j, :], j == 0, j == 21,
                     [wd.b[0], act.b[j]], [pd.b[0]])
            s.tt("dve", o.t[:, m, :], pd.t[:, 0:TT], x.t[:, m, :], ALU.add, [pd.b[0], x.b[0]], [o.b[0]])
            if m == 0 and it + 1 < NTL:
                norm_post(it + 1)
        if not final:
            s.dma("pool", rsrc[:, :, it * TT:(it + 1) * TT], o.t[:], reads=[o.b[0]])
    if final:
        fin_stat(NTL - 1)
    s.pop()


IN_SHAPES = {
    "xT": ([1024, S], F32), "memT": ([1024, 256], F32), "pos": ([1, S], I32),
    "perm64": ([128, 128], F32), "perm96": ([128, 128], F32), "invf64": ([128, 1], F32), "invf96": ([128, 1], F32),
    "l0_mix_norm": ([128, 8], F32), "l0_w_in": ([1024, 1632], F32), "l0_sinks": ([128, 8], F32),
    "l0_q_norm": ([128, 3], F32), "l0_w_uq": ([384, 768], F32), "l0_kv_norm": ([128, 2], F32),
    "l0_w_ukv": ([256, 1024], F32), "l0_w_out": ([1024, 1024], F32),
    "l1_mix_norm": ([128, 8], F32), "l1_w_qkv": ([1024, 3072], F32), "l1_w_out": ([1024, 1024], F32),
    "final_norm": ([128, 8], F32),
}
for _L in (0, 1):
    _p = "l%d_" % _L
    IN_SHAPES.update({
        _p + "x_norm": ([128, 8], F32), _p + "mem_norm": ([128, 8], F32), _p + "w_xq": ([1024, 512], F32),
        _p + "w_xkv": ([1024, 1024], F32), _p + "w_xo": ([512, 1024], F32), _p + "ffn_norm": ([128, 8], F32),
        _p + "w_gate": ([1024, HID], F32), _p + "w_up": ([1024, HID], F32), _p + "w_down": ([HID, 1024], F32),
    })

SCRATCH = {
    "R": ([1024, S], F32), "QA": ([4, 128, S], BF16), "KA": ([2, 128, S], BF16), "VA": ([128, S], BF16),
    "QM": ([8, 96, S], BF16), "KM": ([8, 96, S], BF16), "VM": ([4, 128, S], BF16), "OT": ([8, 128, S], BF16),
    "Q1": ([8, 128, S], BF16), "K1": ([8, 128, S], BF16), "V1": ([8, 128, S], BF16),
}

ALL_PHASES = ("A0", "B0", "C0", "D0", "E0", "A1", "B1", "D1", "E1")


def build(phases=ALL_PHASES, debug=()):
    nc = bass.Bass("TRN2", target_bir_lowering=False)
    din = {k: nc.dram_tensor(k, sh, dt, kind="ExternalInput").ap() for k, (sh, dt) in IN_SHAPES.items()}
    dsc = {}
    for k, (sh, dt) in SCRATCH.items():
        kind = "ExternalOutput" if k in debug else "Internal"
        dsc[k] = nc.dram_tensor("sc_" + k, sh, dt, kind=kind).ap()
    yT = nc.dram_tensor("yT", [1024, S], F32, kind="ExternalOutput").ap()
    dsc["_yT"] = yT
    with ExitStack() as es:
        s = Sched(nc, es)
        C = Ctx()
        consts_common(s, C, din)
        C.eps = s.tile([128, 1], F32, "eps")
        s.memset("dve", C.eps.t[:], EPS, [C.eps.b[0]])
        s.barrier()
        for ph in phases:
            if ph == "A0":
                phase_A0(s, C, din, dsc, din["xT"])
            elif ph == "B0":
                phase_band(s, C, din, dsc, 0)
            elif ph == "C0":
                phase_mla(s, C, din, dsc)
            elif ph == "D0":
                phase_D(s, C, din, dsc, 0, din["xT"])
            elif ph == "E0":
                phase_E(s, C, din, dsc, 0, False, None)
            elif ph == "A1":
                phase_A1(s, C, din, dsc, dsc["R"])
            elif ph == "B1":
                phase_band(s, C, din, dsc, 1)
            elif ph == "D1":
                phase_D(s, C, din, dsc, 1, dsc["R"])
            elif ph == "E1":
                phase_E(s, C, din, dsc, 1, True, yT)
            s.barrier()
        s.emit()
    return nc


def _colvec(g):
    g = np.asarray(g, np.float32)
    return np.ascontiguousarray(g.reshape(-1, 128).T)


def _consts():
    c = {}
    p64 = np.zeros((128, 128), np.float32)
    for m in range(128):
        j = m % 64
        if j < 32:
            p64[m + 32, m] = -1.0
        else:
            p64[m - 32, m] = 1.0
    c["perm64"] = p64
    p96 = np.zeros((128, 128), np.float32)
    for j in range(32):
        m = 64 + j
        if j < 16:
            p96[m + 16, m] = -1.0
        else:
            p96[m - 16, m] = 1.0
    c["perm96"] = p96
    f64 = (np.float32(10000.0) ** (-np.arange(0, 64, 2, dtype=np.float32) / np.float32(64))).astype(np.float32)
    f32 = (np.float32(10000.0) ** (-np.arange(0, 32, 2, dtype=np.float32) / np.float32(32))).astype(np.float32)
    c["invf64"] = np.ascontiguousarray(f64[np.arange(128) % 32][:, None])
    v = np.zeros((128, 1), np.float32)
    v[64:96, 0] = f32[np.arange(32) % 16]
    c["invf96"] = v
    return c


def make_in_maps(inputs):
    shared = _consts()
    w_in = np.asarray(inputs["l0_w_in"], np.float32)
    qa, ka, va, cqw, ckvw, krw = np.split(w_in, [512, 640, 768, 1152, 1408], axis=1)
    ka_dup = np.concatenate([ka[:, 0:64], ka[:, 0:64], ka[:, 64:128], ka[:, 64:128]], axis=1)
    krp = np.concatenate([np.zeros((1024, 64), np.float32), krw], axis=1)
    shared["l0_w_in"] = np.ascontiguousarray(np.concatenate([qa, ka_dup, va, cqw, ckvw, krp], axis=1))
    shared["l0_w_uq"] = np.ascontiguousarray(inputs["l0_w_uq"], np.float32)
    wukv = np.asarray(inputs["l0_w_ukv"], np.float32).reshape(256, 8, 128)
    shared["l0_w_ukv"] = np.ascontiguousarray(
        np.concatenate([wukv[:, :, 0:64].reshape(256, 512), wukv[:, :, 64:128].reshape(256, 512)], axis=1))
    shared["l0_sinks"] = np.ascontiguousarray(np.broadcast_to(np.asarray(inputs["l0_sinks"], np.float32)[None, :],
                                                              (128, 8)))
    for k in ("l0_mix_norm", "l0_q_norm", "l0_kv_norm", "l1_mix_norm", "final_norm", "l0_x_norm", "l0_mem_norm",
              "l0_ffn_norm", "l1_x_norm", "l1_mem_norm", "l1_ffn_norm"):
        shared[k] = _colvec(inputs[k])
    for k in ("l0_w_out", "l1_w_qkv", "l1_w_out", "l0_w_xq", "l0_w_xkv", "l0_w_xo", "l0_w_gate", "l0_w_up",
              "l0_w_down", "l1_w_xq", "l1_w_xkv", "l1_w_xo", "l1_w_gate", "l1_w_up", "l1_w_down"):
        shared[k] = np.ascontiguousarray(inputs[k], np.float32)
    maps = []
    x = np.asarray(inputs["x"], np.float32)
    mem = np.asarray(inputs["mem"], np.float32)
    pos = np.asarray(inputs["positions"], np.int32)
    for b in range(NCORE):
        m = dict(shared)
        m["xT"] = np.ascontiguousarray(x[b].T)
        m["memT"] = np.ascontiguousarray(mem[b].T)
        m["pos"] = np.ascontiguousarray(pos[b][None, :])
        maps.append(m)
    return maps


_NC_CACHE = {}


def kernel(**inputs):
    if "nc" not in _NC_CACHE:
        _NC_CACHE["nc"] = build()
    nc = _NC_CACHE["nc"]
    maps = make_in_maps(inputs)
    res = run_bass_kernel_spmd(nc, maps, core_ids=list(range(NCORE)))
    out = np.empty((NCORE, S, D), np.float32)
    for b in range(NCORE):
        out[b] = np.asarray(res.results[b]["yT"], np.float32).T
    return out
```

```python
import math
from contextlib import ExitStack

import numpy as np
import concourse.bass as bass
import concourse.mybir as mybir
from concourse.bass_utils import run_bass_kernel_spmd

F32 = mybir.dt.float32
BF16 = mybir.dt.bfloat16
I32 = mybir.dt.int32
ALU = mybir.AluOpType
AF = mybir.ActivationFunctionType

S = 8192
D = 1024
NCORE = 8
HID = 2816
EPS = 1e-6
SB_BASE = 16512
SB_END = 229376
NDMA = 8
PI = math.pi
CW1 = 6.28125
CW2 = float(np.float32(2 * math.pi - CW1))


class Buf:
    __slots__ = ("name", "w", "r", "psum")

    def __init__(self, name="", psum=False):
        self.name = name
        self.w = None
        self.r = {}
        self.psum = psum


class EngQ:
    def __init__(self, name):
        self.name = name
        self.count = 0
        self.waited = {}
        self.prog = []
        self.dsem_i = 0


class Tile:
    def __init__(self, t, nb, name):
        self.t = t
        self.b = [Buf(name) for _ in range(nb)]

    def __getitem__(self, k):
        return self.t[k]


class Sched:
    def __init__(self, nc, es):
        self.nc = nc
        self.q = {n: EngQ(n) for n in ("pe", "act", "dve", "pool", "sp")}
        self.sems = {}
        for n in self.q:
            self.sems[n] = es.enter_context(nc.semaphore("c_" + n))
        self.dcount = {}
        for qn in ("sp", "pool"):
            for i in range(NDMA):
                k = "d_%s%d" % (qn, i)
                self.sems[k] = es.enter_context(nc.semaphore(k))
                self.dcount[k] = 0
        self.sb_off = SB_BASE
        self.sb_mark = []
        self.nalloc = 0
        self.psum = []
        for i in range(8):
            t = Tile(nc.alloc_psum_tensor("ps%d" % i, [128, 512], F32), 1, "ps%d" % i)
            t.b[0].psum = True
            self.psum.append(t)

    def tile(self, shape, dtype, name="t", nb=1):
        self.nalloc += 1
        esz = 4 if dtype in (F32, I32) else 2
        n = 1
        for x in shape[1:]:
            n *= x
        nbytes = (n * esz + 63) // 64 * 64
        off = self.sb_off
        assert off + nbytes <= SB_END, "SBUF overflow at %s: need %d more" % (name, off + nbytes - SB_END)
        self.sb_off += nbytes
        h = self.nc.alloc_sbuf_tensor_at("%s_%d" % (name, self.nalloc), list(shape), dtype, offset=off)
        return Tile(h, nb, name)

    def push(self):
        self.sb_mark.append(self.sb_off)

    def pop(self):
        self.sb_off = self.sb_mark.pop()

    def _deps(self, E, reads, writes):
        deps = {}

        def add(d, same_ok):
            if d is None:
                return
            k, v = d
            if k == E.name and not same_ok:
                return
            if deps.get(k, 0) < v:
                deps[k] = v
        same = E.name != "pe"
        for b in reads:
            add(b.w, same)
            if b.psum:
                for k, v in b.r.items():
                    add((k, v), False)
        for b in writes:
            add(b.w, same)
            for k, v in b.r.items():
                add((k, v), same)
        waits = []
        for k, v in deps.items():
            if E.waited.get(k, 0) < v:
                E.waited[k] = v
                waits.append((k, v))
        return waits

    def _done(self, reads, writes, done):
        k, v = done
        for b in reads:
            if b.r.get(k, 0) < v:
                b.r[k] = v
        for b in writes:
            b.w = done
            b.r = {}

    def op(self, eng, fn, reads=(), writes=()):
        E = self.q[eng]
        waits = self._deps(E, reads, writes)
        E.count += 1
        E.prog.append((waits, fn, (eng, 1)))
        self._done(reads, writes, (eng, E.count))

    def dma(self, eng, out, in_, reads=(), writes=()):
        E = self.q[eng]
        k = "d_%s%d" % (eng, E.dsem_i % NDMA)
        E.dsem_i += 1
        waits = self._deps(E, reads, writes)
        prev = self.dcount[k]
        if prev > 0 and E.waited.get(k, 0) < prev:
            E.waited[k] = prev
            waits.append((k, prev))
        self.dcount[k] = prev + 16
        E.prog.append((waits, lambda e, o=out, i=in_: e.dma_start(out=o, in_=i), (k, 16)))
        self._done(reads, writes, (k, prev + 16))

    def barrier(self):
        allv = [(n, E.count) for n, E in self.q.items() if E.count]
        allv += [(k, v) for k, v in self.dcount.items() if v]
        for n, E in self.q.items():
            waits = []
            for k, v in allv:
                if k != n and E.waited.get(k, 0) < v:
                    E.waited[k] = v
                    waits.append((k, v))
            if waits:
                E.prog.append((waits, None, None))

    def emit(self):
        nc = self.nc
        sems = self.sems
        self.barrier()
        with nc.Block() as block:
            def run(E, e):
                for waits, fn, inc in E.prog:
                    for k, v in waits:
                        e.wait_ge(sems[k], v)
                    if fn is not None:
                        fn(e).then_inc(sems[inc[0]], inc[1])

            @block.tensor
            def _(e):
                run(self.q["pe"], e)

            @block.scalar
            def _(e):
                run(self.q["act"], e)

            @block.vector
            def _(e):
                run(self.q["dve"], e)

            @block.gpsimd
            def _(e):
                run(self.q["pool"], e)

            @block.sync
            def _(e):
                run(self.q["sp"], e)

    def mm(self, out, lhsT, rhs, start, stop, reads, writes):
        self.op("pe", lambda e: e.matmul(out, lhsT, rhs, start=start, stop=stop), reads, writes)

    def tr(self, out, in_, ident, reads, writes):
        self.op("pe", lambda e: e.transpose(out, in_, ident), reads, writes)

    def act(self, out, in_, func, reads, writes, bias=0.0, scale=1.0):
        self.op("act", lambda e: e.activation(out=out, in_=in_, func=func, bias=bias, scale=scale), reads, writes)

    def cp(self, eng, out, in_, reads, writes):
        if eng == "act":
            self.act(out, in_, AF.Copy, reads, writes)
        else:
            self.op(eng, lambda e: e.tensor_copy(out=out, in_=in_), reads, writes)

    def tt(self, eng, out, in0, in1, op, reads, writes):
        self.op(eng, lambda e: e.tensor_tensor(out=out, in0=in0, in1=in1, op=op), reads, writes)

    def ts(self, eng, out, in0, s1, s2, op0, op1, reads, writes):
        if s2 is None:
            self.op(eng, lambda e: e.tensor_scalar(out=out, in0=in0, scalar1=s1, scalar2=None, op0=op0), reads, writes)
        else:
            self.op(eng, lambda e: e.tensor_scalar(out=out, in0=in0, scalar1=s1, scalar2=s2, op0=op0, op1=op1),
                    reads, writes)

    def stt(self, out, in0, scalar, in1, op0, op1, reads, writes):
        self.op("dve", lambda e: e.scalar_tensor_tensor(out=out, in0=in0, scalar=scalar, in1=in1, op0=op0, op1=op1),
                reads, writes)

    def recip(self, out, in_, reads, writes):
        self.op("dve", lambda e: e.reciprocal(out=out, in_=in_), reads, writes)

    def memset(self, eng, ap, val, writes):
        self.op(eng, lambda e: e.memset(ap, val), (), writes)


class Ctx:
    pass


DBG = {"nt": None, "stop": 99}


def load_weight(s, stg, dram, K, N, name, conv_engs=("act", "dve")):
    KC = (K + 127) // 128
    W = s.tile([128, KC, N], BF16, name)
    CH = 1024
    i = 0
    for kc in range(KC):
        rows = min(128, K - kc * 128)
        for c0 in range(0, N, CH):
            c1 = min(N, c0 + CH)
            st = stg[s.stg_i % len(stg)]
            s.stg_i += 1
            s.dma("sp", st.t[0:rows, 0:c1 - c0], dram[kc * 128:kc * 128 + rows, c0:c1], writes=[st.b[0]])
            eng = conv_engs[i % len(conv_engs)]
            i += 1
            s.cp(eng, W.t[0:rows, kc, c0:c1], st.t[0:rows, 0:c1 - c0], [st.b[0]], [W.b[0]])
    return W


def load_small(s, dram_ap, shape, dtype, name):
    t = s.tile(shape, dtype, name)
    s.dma("sp", t.t[:], dram_ap, writes=[t.b[0]])
    return t


def consts_common(s, C, din):
    C.stg = [s.tile([128, 1024], F32, "stg") for _ in range(2)]
    s.stg_i = 0
    io = s.tile([128, 512], F32, "iota")
    s.op("pool", lambda e: e.iota(io.t[:], pattern=[[1, 512]], base=0, channel_multiplier=-1,
                                  allow_small_or_imprecise_dtypes=True), (), [io.b[0]])
    C.io = io
    C.ident = s.tile([128, 128], BF16, "ident")
    s.ts("dve", C.ident.t[:], io.t[:, 0:128], 0.0, None, ALU.is_equal, None, [io.b[0]], [C.ident.b[0]])
    C.ones = {}
    for n in (1024, 384, 256):
        o = s.tile([128, 128], BF16, "ones%d" % n)
        s.memset("dve", o.t[:], 1.0 / n, [o.b[0]])
        C.ones[n] = o
    o = s.tile([128, 128], BF16, "ones1")
    s.memset("dve", o.t[:], 1.0, [o.b[0]])
    C.ones[1] = o
    for nm in ("perm64", "perm96"):
        st = load_small(s, din[nm][:, :], [128, 128], F32, nm + "f")
        p = s.tile([128, 128], BF16, nm)
        s.cp("dve", p.t[:], st.t[:], [st.b[0]], [p.b[0]])
        setattr(C, nm, p)
    C.invf64 = load_small(s, din["invf64"][:, :], [128, 1], F32, "invf64")
    C.invf96 = load_small(s, din["invf96"][:, :], [128, 1], F32, "invf96")


def rope_tables(s, C, posf, invf, Ct, St, tmp, rows=128):
    ang, kt, ki, rr, mt = tmp
    R = slice(0, rows)
    s.act(ang.t[R], posf.t[R], AF.Copy, [posf.b[0], invf.b[0]], [ang.b[0]], scale=invf.t[R, 0:1])
    s.act(kt.t[R], ang.t[R], AF.Copy, [ang.b[0]], [kt.b[0]], scale=1.0 / (2 * PI))
    s.cp("dve", ki.t[R], kt.t[R], [kt.b[0]], [ki.b[0]])
    s.cp("dve", kt.t[R], ki.t[R], [ki.b[0]], [kt.b[0]])
    s.stt(rr.t[R], kt.t[R], -CW1, ang.t[R], ALU.mult, ALU.add, [kt.b[0], ang.b[0]], [rr.b[0]])
    s.stt(rr.t[R], kt.t[R], -CW2, rr.t[R], ALU.mult, ALU.add, [kt.b[0], rr.b[0]], [rr.b[0]])
    s.ts("dve", ang.t[R], rr.t[R], -3.1415925, 3.1415925, ALU.max, ALU.min, [rr.b[0]], [ang.b[0]])
    s.act(St.t[R], ang.t[R], AF.Sin, [ang.b[0]], [St.b[0]])
    s.ts("dve", mt.t[R], rr.t[R], PI / 2, PI, ALU.add, ALU.is_gt, [rr.b[0]], [mt.b[0]])
    s.stt(kt.t[R], mt.t[R], -2 * PI, rr.t[R], ALU.mult, ALU.add, [mt.b[0], rr.b[0]], [kt.b[0]])
    s.act(Ct.t[R], kt.t[R], AF.Sin, [kt.b[0]], [Ct.b[0]], bias=PI / 2)


class PsView:
    def __init__(self, bank, half):
        self.bank = bank
        self.half = half
        self.b = [Buf("psh", psum=True)]

    @property
    def t(self):
        return self.bank.t[:, self.half * 256:(self.half + 1) * 256]


class PsPool:
    def __init__(self, s, idx, halves=False):
        if halves:
            self.tiles = [PsView(s.psum[i], h) for i in idx for h in range(2)]
        else:
            self.tiles = [s.psum[i] for i in idx]
        self.i = 0

    def get(self):
        t = self.tiles[self.i % len(self.tiles)]
        self.i += 1
        return t


def rms_stat(s, C, n_feat, sq, kcs, rstd, psp, ncol, lnexp=False):
    ps = psp.get()
    for i, kc in enumerate(kcs):
        s.mm(ps.t[:, 0:ncol], C.ones[n_feat].t[:], sq.t[:, kc, 0:ncol], i == 0, i == len(kcs) - 1,
             [C.ones[n_feat].b[0], sq.b[0]], [ps.b[0]])
    if lnexp:
        s.act(rstd.t[:, 0:ncol], ps.t[:, 0:ncol], AF.Ln, [ps.b[0], C.eps.b[0]], [rstd.b[0]], bias=C.eps.t[:, 0:1])
        s.act(rstd.t[:, 0:ncol], rstd.t[:, 0:ncol], AF.Exp, [rstd.b[0]], [rstd.b[0]], scale=-0.5)
        return
    s.act(rstd.t[:, 0:ncol], ps.t[:, 0:ncol], AF.Sqrt, [ps.b[0], C.eps.b[0]], [rstd.b[0]], bias=C.eps.t[:, 0:1])
    s.recip(rstd.t[:, 0:ncol], rstd.t[:, 0:ncol], [rstd.b[0]], [rstd.b[0]])


def rope_chunk(s, C, ps, rows, perm, Ct, St, zb, t1, t2, out_ap, out_bufs, psp, pend):
    R = slice(0, rows)
    s.cp("act", zb.t[R], ps.t[R], [ps.b[0]], [zb.b[0]])

    def rest():
        pr = psp.get()
        s.mm(pr.t[R], perm.t[R, 0:rows], zb.t[R], True, True, [perm.b[0], zb.b[0]], [pr.b[0]])
        s.tt("dve", t1.t[R], ps.t[R], Ct.t[R], ALU.mult, [ps.b[0], Ct.b[0]], [t1.b[0]])
        s.tt("dve", t2.t[R], pr.t[R], St.t[R], ALU.mult, [pr.b[0], St.b[0]], [t2.b[0]])
        s.tt("dve", out_ap, t1.t[R], t2.t[R], ALU.add, [t1.b[0], t2.b[0]], out_bufs)
    pend.append(rest)


def flush(pend, keep=0):
    while len(pend) > keep:
        pend.pop(0)()


def phase_A0(s, C, din, dsc, x_src):
    s.push()
    TT = 512
    w_in = load_weight(s, C.stg, din["l0_w_in"], 1024, 1632, "w_in")
    w_uq = load_weight(s, C.stg, din["l0_w_uq"], 384, 768, "w_uq")
    w_ukv = load_weight(s, C.stg, din["l0_w_ukv"], 256, 1024, "w_ukv")
    g_mix = load_small(s, din["l0_mix_norm"][:, :], [128, 8], F32, "g_mix")
    g_q = load_small(s, din["l0_q_norm"][:, :], [128, 3], F32, "g_q")
    g_kv = load_small(s, din["l0_kv_norm"][:, :], [128, 2], F32, "g_kv")
    xt = s.tile([128, 8, TT], F32, "xt")
    sq = s.tile([128, 8, TT], BF16, "sq")
    xns = [s.tile([128, 8, TT], BF16, "xn") for _ in range(2)]
    rstds = [s.tile([128, TT], F32, "rstd") for _ in range(2)]
    posi = s.tile([128, TT], I32, "posi")
    posf = s.tile([128, TT], F32, "posf")
    tmp = [s.tile([128, TT], F32, "rt%d" % i) for i in range(3)]
    tmp = [tmp[0], tmp[1], s.tile([128, TT], I32, "rki"), tmp[2], s.tile([128, TT], F32, "rmt")]
    tmpb = tmp
    C64 = s.tile([128, TT], F32, "C64")
    S64 = s.tile([128, TT], F32, "S64")
    C96 = s.tile([128, TT], F32, "C96")
    S96 = s.tile([128, TT], F32, "S96")
    zb = [s.tile([128, TT], BF16, "zb") for _ in range(3)]
    t1 = [s.tile([128, TT], F32, "t1") for _ in range(3)]
    t2 = [s.tile([128, TT], F32, "t2") for _ in range(3)]
    cq = s.tile([128, 3, TT], F32, "cq")
    cqs = s.tile([128, 3, TT], BF16, "cqs")
    cqn = s.tile([128, 3, TT], BF16, "cqn")
    ckv = s.tile([128, 2, TT], F32, "ckv")
    ckvs = s.tile([128, 2, TT], BF16, "ckvs")
    ckvn = s.tile([128, 2, TT], BF16, "ckvn")
    rq = s.tile([128, TT], F32, "rq")
    rkv = s.tile([128, TT], F32, "rkv")
    krr = s.tile([128, TT], BF16, "krr")
    NB = 2
    o_qa = [s.tile([128, 4, TT], BF16, "o_qa") for _ in range(NB)]
    o_ka = [s.tile([128, 2, TT], BF16, "o_ka") for _ in range(NB)]
    o_va = [s.tile([128, TT], BF16, "o_va") for _ in range(NB)]
    o_qm = [s.tile([128, 8, TT], BF16, "o_qm") for _ in range(NB)]
    o_km = [s.tile([128, 8, TT], BF16, "o_km") for _ in range(NB)]
    o_vm = [s.tile([128, 4, TT], BF16, "o_vm") for _ in range(NB)]
    psp = PsPool(s, range(8))
    xsrc = x_src.rearrange("(c p) t -> p c t", p=128)
    ri = 0
    pend = []
    NTL = DBG["nt"] or S // TT

    def front_a(it):
        s.dma("sp", xt.t[:], xsrc[:, :, it * TT:(it + 1) * TT], writes=[xt.b[0]])
        s.act(sq.t[:], xt.t[:], AF.Square, [xt.b[0]], [sq.b[0]])

    def front_b(it):
        xn_, rstd_ = xns[it % 2], rstds[it % 2]
        rms_stat(s, C, 1024, sq, range(8), rstd_, psp, TT)
        for c in range(8):
            s.stt(xn_.t[:, c, :], xt.t[:, c, :], g_mix.t[:, c:c + 1], rstd_.t[:], ALU.mult, ALU.mult,
                  [xt.b[0], g_mix.b[0], rstd_.b[0]], [xn_.b[0]])

    front_a(0)
    front_b(0)
    for it in range(NTL):
        t0 = it * TT
        bi = it % NB
        xn = xns[it % 2]
        s.dma("sp", posi.t[:], din["pos"][0:1, t0:t0 + TT].partition_broadcast(128), writes=[posi.b[0]])
        s.cp("dve", posf.t[:], posi.t[:], [posi.b[0]], [posf.b[0]])
        rope_tables(s, C, posf, C.invf64, C64, S64, tmp)
        rope_tables(s, C, posf, C.invf96, C96, S96, tmpb)

        def proj(W, kcs, col0, M, src, ps, keep=0):
            for i, kc in enumerate(kcs):
                s.mm(ps.t[0:M, :], W.t[:, kc, col0:col0 + M], src.t[:, kc, :], i == 0, i == len(kcs) - 1,
                     [W.b[0], src.b[0]], [ps.b[0]])
            flush(pend, keep=keep)
        for j in range(6):
            ps = psp.get()
            proj(w_in, range(8), j * 128, 128, xn, ps, keep=1)
            if j < 4:
                oap, ob = o_qa[bi].t[:, j, :], o_qa[bi].b
            else:
                oap, ob = o_ka[bi].t[:, j - 4, :], o_ka[bi].b
            rope_chunk(s, C, ps, 128, C.perm64, C64, S64, zb[ri % 3], t1[ri % 3], t2[ri % 3], oap, ob, psp, pend)
            ri += 1
            if j == 1 and it + 1 < NTL:
                front_a(it + 1)
        ps = psp.get()
        proj(w_in, range(8), 768, 128, xn, ps)
        s.cp("act", o_va[bi].t[:], ps.t[:], [ps.b[0]], o_va[bi].b)
        if it + 1 < NTL:
            front_b(it + 1)
        for j in range(3):
            ps = psp.get()
            proj(w_in, range(8), 896 + j * 128, 128, xn, ps)
            s.cp("act", cq.t[:, j, :], ps.t[:], [ps.b[0]], [cq.b[0]])
        s.act(cqs.t[:], cq.t[:], AF.Square, [cq.b[0]], [cqs.b[0]])
        rms_stat(s, C, 384, cqs, range(3), rq, psp, TT)
        for c in range(3):
            s.stt(cqn.t[:, c, :], cq.t[:, c, :], g_q.t[:, c:c + 1], rq.t[:], ALU.mult, ALU.mult,
                  [cq.b[0], g_q.b[0], rq.b[0]], [cqn.b[0]])
        for j in range(2):
            ps = psp.get()
            proj(w_in, range(8), 1280 + j * 128, 128, xn, ps)
            s.cp("act", ckv.t[:, j, :], ps.t[:], [ps.b[0]], [ckv.b[0]])
        s.act(ckvs.t[:], ckv.t[:], AF.Square, [ckv.b[0]], [ckvs.b[0]])
        rms_stat(s, C, 256, ckvs, range(2), rkv, psp, TT)
        for c in range(2):
            s.stt(ckvn.t[:, c, :], ckv.t[:, c, :], g_kv.t[:, c:c + 1], rkv.t[:], ALU.mult, ALU.mult,
                  [ckv.b[0], g_kv.b[0], rkv.b[0]], [ckvn.b[0]])
        ps = psp.get()
        proj(w_in, range(8), 1536, 96, xn, ps)
        rope_chunk(s, C, ps, 96, C.perm96, C96, S96, zb[ri % 3], t1[ri % 3], t2[ri % 3], krr.t[0:96, :], [krr.b[0]],
                   psp, pend)
        ri += 1
        for h in range(8):
            ps = psp.get()
            proj(w_uq, range(3), h * 96, 96, cqn, ps, keep=1)
            rope_chunk(s, C, ps, 96, C.perm96, C96, S96, zb[ri % 3], t1[ri % 3], t2[ri % 3],
                       o_qm[bi].t[0:96, h, :], o_qm[bi].b, psp, pend)
            ri += 1
        for h in range(8):
            ps = psp.get()
            proj(w_ukv, range(2), h * 64, 64, ckvn, ps)
            s.cp("act", o_km[bi].t[0:64, h, :], ps.t[0:64, :], [ps.b[0]], o_km[bi].b)
            s.cp("act", o_km[bi].t[64:96, h, :], krr.t[64:96, :], [krr.b[0]], o_km[bi].b)
        for j in range(4):
            ps = psp.get()
            proj(w_ukv, range(2), 512 + j * 128, 128, ckvn, ps)
            s.cp("act", o_vm[bi].t[:, j, :], ps.t[:], [ps.b[0]], o_vm[bi].b)
        flush(pend)
        sl = slice(t0, t0 + TT)
        s.dma("pool", dsc["QA"].rearrange("c p t -> p c t")[:, :, sl], o_qa[bi].t[:], reads=o_qa[bi].b)
        s.dma("pool", dsc["KA"].rearrange("c p t -> p c t")[:, :, sl], o_ka[bi].t[:], reads=o_ka[bi].b)
        s.dma("pool", dsc["VA"][:, sl], o_va[bi].t[:], reads=o_va[bi].b)
        s.dma("pool", dsc["QM"].rearrange("c p t -> p c t")[:, :, sl], o_qm[bi].t[0:96], reads=o_qm[bi].b)
        s.dma("pool", dsc["KM"].rearrange("c p t -> p c t")[:, :, sl], o_km[bi].t[0:96], reads=o_km[bi].b)
        s.dma("pool", dsc["VM"].rearrange("c p t -> p c t")[:, :, sl], o_vm[bi].t[:], reads=o_vm[bi].b)
    s.pop()


def phase_A1(s, C, din, dsc, x_src):
    s.push()
    TT = 512
    w = load_weight(s, C.stg, din["l1_w_qkv"], 1024, 3072, "w_qkv")
    g_mix = load_small(s, din["l1_mix_norm"][:, :], [128, 8], F32, "g_mix")
    xt = s.tile([128, 8, TT], F32, "xt")
    sq = s.tile([128, 8, TT], BF16, "sq")
    xns = [s.tile([128, 8, TT], BF16, "xn") for _ in range(2)]
    rstds = [s.tile([128, TT], F32, "rstd") for _ in range(2)]
    posi = s.tile([128, TT], I32, "posi")
    posf = s.tile([128, TT], F32, "posf")
    tmp = [s.tile([128, TT], F32, "rt%d" % i) for i in range(3)]
    tmp = [tmp[0], tmp[1], s.tile([128, TT], I32, "rki"), tmp[2], s.tile([128, TT], F32, "rmt")]
    C64 = s.tile([128, TT], F32, "C64")
    S64 = s.tile([128, TT], F32, "S64")
    zb = [s.tile([128, TT], BF16, "zb") for _ in range(3)]
    t1 = [s.tile([128, TT], F32, "t1") for _ in range(3)]
    t2 = [s.tile([128, TT], F32, "t2") for _ in range(3)]
    NB = 2
    o_q = [s.tile([128, 8, TT], BF16, "o_q") for _ in range(NB)]
    o_k = [s.tile([128, 8, TT], BF16, "o_k") for _ in range(NB)]
    o_v = [s.tile([128, 8, TT], BF16, "o_v") for _ in range(NB)]
    psp = PsPool(s, range(8))
    xsrc = x_src.rearrange("(c p) t -> p c t", p=128)
    ri = 0
    pend = []
    NTL = S // TT

    def front_a(it):
        s.dma("sp", xt.t[:], xsrc[:, :, it * TT:(it + 1) * TT], writes=[xt.b[0]])
        s.act(sq.t[:], xt.t[:], AF.Square, [xt.b[0]], [sq.b[0]])

    def front_b(it):
        xn_, rstd_ = xns[it % 2], rstds[it % 2]
        rms_stat(s, C, 1024, sq, range(8), rstd_, psp, TT)
        for c in range(8):
            s.stt(xn_.t[:, c, :], xt.t[:, c, :], g_mix.t[:, c:c + 1], rstd_.t[:], ALU.mult, ALU.mult,
                  [xt.b[0], g_mix.b[0], rstd_.b[0]], [xn_.b[0]])

    front_a(0)
    front_b(0)
    for it in range(NTL):
        t0 = it * TT
        bi = it % NB
        xn = xns[it % 2]
        s.dma("sp", posi.t[:], din["pos"][0:1, t0:t0 + TT].partition_broadcast(128), writes=[posi.b[0]])
        s.cp("dve", posf.t[:], posi.t[:], [posi.b[0]], [posf.b[0]])
        rope_tables(s, C, posf, C.invf64, C64, S64, tmp)
        for j in range(24):
            if j == 2 and it + 1 < NTL:
                front_a(it + 1)
            if j == 8 and it + 1 < NTL:
                front_b(it + 1)
            ps = psp.get()
            for i in range(8):
                s.mm(ps.t[:], w.t[:, i, j * 128:(j + 1) * 128], xn.t[:, i, :], i == 0, i == 7,
                     [w.b[0], xn.b[0]], [ps.b[0]])
            flush(pend, keep=1 if j < 16 else 0)
            if j < 16:
                o = o_q[bi] if j < 8 else o_k[bi]
                rope_chunk(s, C, ps, 128, C.perm64, C64, S64, zb[ri % 3], t1[ri % 3], t2[ri % 3],
                           o.t[:, j % 8, :], o.b, psp, pend)
                ri += 1
            else:
                s.cp("act", o_v[bi].t[:, j - 16, :], ps.t[:], [ps.b[0]], o_v[bi].b)
        flush(pend)
        sl = slice(t0, t0 + TT)
        s.dma("pool", dsc["Q1"].rearrange("c p t -> p c t")[:, :, sl], o_q[bi].t[:], reads=o_q[bi].b)
        s.dma("pool", dsc["K1"].rearrange("c p t -> p c t")[:, :, sl], o_k[bi].t[:], reads=o_k[bi].b)
        s.dma("pool", dsc["V1"].rearrange("c p t -> p c t")[:, :, sl], o_v[bi].t[:], reads=o_v[bi].b)
    s.pop()


def make_band_masks(s, C):
    C.mask = {}
    for nm, op2 in (("dil", ALU.is_le), ("swa", ALU.is_lt)):
        mf = s.tile([128, 256], F32, "mf" + nm)
        s.ts("dve", mf.t[:, 0:128], C.io.t[:, 0:128], 0.0, None, ALU.is_ge, None, [C.io.b[0]], [mf.b[0]])
        s.ts("dve", mf.t[:, 128:256], C.io.t[:, 0:128], 0.0, None, op2, None, [C.io.b[0]], [mf.b[0]])
        m = s.tile([128, 256], BF16, "mask" + nm)
        s.cp("dve", m.t[:], mf.t[:], [mf.b[0]], [m.b[0]])
        C.mask[nm] = m


def phase_band(s, C, din, dsc, layer):
    s.push()
    make_band_masks(s, C)
    if layer == 0:
        nchunk, patterns, mask, scale = 4, (1,), C.mask["swa"], 64 ** -0.5
        sinks = load_small(s, din["l0_sinks"][:, :], [128, 8], F32, "sinks")
        es = s.tile([128, 8], F32, "es")
        s.act(es.t[:], sinks.t[:], AF.Exp, [sinks.b[0]], [es.b[0]])
    else:
        nchunk, patterns, mask, scale = 8, (1, 4, 16), C.mask["dil"], 64 ** -0.5
        es = None
    Q = s.tile([128, S], BF16, "Q")
    K = s.tile([128, S], BF16, "K")
    VT = s.tile([128, S], BF16, "VT")
    ACC = [s.tile([128, S], F32, "ACC") for _ in range(2)]
    VA = [s.tile([128, 64, 128], BF16, "Vaug") for _ in range(2)]
    for v in VA:
        s.memset("pool", v.t[:, :, 64:128], 1.0, v.b)
    OST = s.tile([128, S], BF16, "OST")
    NP = 10
    pT = [s.tile([128, 256], BF16, "pT") for _ in range(NP)]
    rd = s.tile([128, 2048], F32, "rd")
    rd0 = s.tile([128, 2048], F32, "rd0")
    psS = PsPool(s, (0, 1, 2, 3))
    psO = PsPool(s, (4, 5))
    psT = PsPool(s, (6, 7))
    pi = 0
    for c in range(nchunk):
        if layer == 0:
            g = c // 2
            s.dma("sp", Q.t[:], dsc["QA"][c], writes=[Q.b[0]])
            if c % 2 == 0:
                s.dma("sp", K.t[:], dsc["KA"][g], writes=[K.b[0]])
                if c == 0:
                    s.dma("sp", VT.t[:], dsc["VA"][:, :], writes=[VT.b[0]])
        else:
            s.dma("sp", Q.t[:], dsc["Q1"][c], writes=[Q.b[0]])
            s.dma("sp", K.t[:], dsc["K1"][c], writes=[K.b[0]])
            s.dma("sp", VT.t[:], dsc["V1"][c], writes=[VT.b[0]])
        for dil in patterns:
            nblk = 64 // dil
            Qv = Q.t[:].rearrange("p (i r) -> p r i", r=dil)
            Kv = K.t[:].rearrange("p (i r) -> p r i", r=dil)
            Vv = VT.t[:].rearrange("p (i r) -> p r i", r=dil)
            if not (layer == 0 and c % 2 == 1):
                for b8 in range(8):
                    pt = psT.get()
                    ptb = pt.t[:].bitcast(BF16)
                    for k in range(8):
                        blk = b8 * 8 + k
                        r, n = blk // nblk, blk % nblk
                        s.tr(ptb[:, k * 128:(k + 1) * 128], Vv[:, r, n * 128:(n + 1) * 128], C.ident.t[:],
                             [VT.b[0], C.ident.b[0]], [pt.b[0]])
                    ptv = ptb.rearrange("p (k c) -> p k c", c=128)
                    for hh in range(2):
                        if layer == 0:
                            col = (c // 2) * 64
                        else:
                            col = hh * 64
                        s.cp("act" if hh == 0 else "dve", VA[hh].t[:, b8 * 8:(b8 + 1) * 8, 0:64],
                             ptv[:, :, col:col + 64], [pt.b[0]], VA[hh].b)
            items = [(hh, r, n) for hh in range(2) for r in range(dil) for n in range(nblk)]
            st = {}
            LOOK = 3

            def stage_qk(item):
                nonlocal pi
                hh, r, n = item
                H = slice(hh * 64, hh * 64 + 64)
                nq = 256 if n + 1 < nblk else 128
                ps = psS.get()
                s.mm(ps.t[:, 0:nq], Kv[H, r, n * 128:(n + 1) * 128], Qv[H, r, n * 128:n * 128 + nq],
                     True, True, [K.b[0], Q.b[0]], [ps.b[0]])
                p = pT[pi % NP]
                pi += 1
                s.act(p.t[:, 0:nq], ps.t[:, 0:nq], AF.Exp, [ps.b[0]], [p.b[0]], scale=scale)
                s.tt("dve", p.t[:, 0:nq], p.t[:, 0:nq], mask.t[:, 0:nq], ALU.mult, [p.b[0], mask.b[0]],
                     [p.b[0]])
                st[item] = (p, nq)

            def stage_pv(item):
                hh, r, n = item
                p, nq = st.pop(item)
                ACCv = ACC[hh].t[:].rearrange("p (i r) -> p r i", r=dil)
                blk = r * nblk + n
                if n == 0:
                    st["po", hh] = psO.get()
                po = st["po", hh]
                sl0 = (n % 4) * 128
                s.mm(po.t[:, sl0:sl0 + 128], VA[hh].t[:, blk, :], p.t[:, 0:128], n == 0, True,
                     [VA[hh].b[0], p.b[0]], [po.b[0]])
                if n % 4 == 3 or n == nblk - 1:
                    m0 = (n // 4) * 4
                    wdt = (n - m0 + 1) * 128
                    dst = ACCv[:, r, m0 * 128:m0 * 128 + wdt]
                    if dil == 1:
                        s.cp("dve", dst, po.t[:, 0:wdt], [po.b[0]], ACC[hh].b)
                    else:
                        s.tt("dve", dst, dst, po.t[:, 0:wdt], ALU.add, [po.b[0], ACC[hh].b[0]], ACC[hh].b)
                if nq == 256:
                    if (n + 1) % 4 == 0:
                        st["po", hh] = psO.get()
                        po = st["po", hh]
                    sl1 = ((n + 1) % 4) * 128
                    s.mm(po.t[:, sl1:sl1 + 128], VA[hh].t[:, blk, :], p.t[:, 128:256], True, False,
                         [VA[hh].b[0], p.b[0]], [po.b[0]])

            for i in range(len(items) + LOOK):
                if i < len(items):
                    stage_qk(items[i])
                if i >= LOOK:
                    stage_pv(items[i - LOOK])
        for hh in range(2):
            for c0 in range(0, S, 2048):
                cs = slice(c0, c0 + 2048)
                if es is not None:
                    hcol = 2 * c + hh
                    s.act(rd.t[64:128, :], ACC[hh].t[64:128, cs], AF.Ln, [ACC[hh].b[0], es.b[0]], [rd.b[0]],
                          bias=es.t[64:128, hcol:hcol + 1])
                else:
                    s.act(rd.t[64:128, :], ACC[hh].t[64:128, cs], AF.Ln, [ACC[hh].b[0]], [rd.b[0]])
                s.act(rd0.t[0:64, :], rd.t[64:128, :], AF.Exp, [rd.b[0]], [rd0.b[0]], scale=-1.0)
                s.tt("dve", OST.t[hh * 64:hh * 64 + 64, cs], ACC[hh].t[0:64, cs], rd0.t[0:64, :], ALU.mult,
                     [ACC[hh].b[0], rd0.b[0]], [OST.b[0]])
        s.dma("pool", dsc["OT"][c], OST.t[:], reads=[OST.b[0]])
    s.pop()


def phase_mla(s, C, din, dsc):
    s.push()
    scale = 96 ** -0.5
    mf = s.tile([128, 128], F32, "mf")
    s.ts("dve", mf.t[:], C.io.t[:, 0:128], 0.0, None, ALU.is_ge, None, [C.io.b[0]], [mf.b[0]])
    tri = s.tile([128, 128], BF16, "tri")
    s.ts("dve", tri.t[:], mf.t[:], 30000.0, -30000.0, ALU.mult, ALU.add, [mf.b[0]], [tri.b[0]])
    Q = [s.tile([128, S], BF16, "Qm") for _ in range(2)]
    K = [s.tile([128, S], BF16, "Km") for _ in range(2)]
    VT = s.tile([128, S], BF16, "VTm")
    VA = [s.tile([128, 64, 128], BF16, "Vaug") for _ in range(2)]
    for v in VA:
        s.memset("pool", v.t[:, :, 64:128], 1.0, v.b)
    OST = s.tile([128, S], BF16, "OST")
    NP = 6
    pT = [s.tile([128, 512], BF16, "pT") for _ in range(NP)]
    rd = [s.tile([128, 512], F32, "rd") for _ in range(2)]
    psS = PsPool(s, (0, 1, 2, 3))
    psO = PsPool(s, (4, 5))
    psT = PsPool(s, (6, 7))
    pi = 0
    for h in range(8):
        hh = h % 2
        s.dma("sp", Q[hh].t[0:96, :], dsc["QM"][h], writes=[Q[hh].b[0]])
        s.dma("sp", K[hh].t[0:96, :], dsc["KM"][h], writes=[K[hh].b[0]])
        if hh == 0:
            s.dma("sp", VT.t[:], dsc["VM"][h // 2], writes=[VT.b[0]])
            for b8 in range(8):
                pt = psT.get()
                ptb = pt.t[:].bitcast(BF16)
                for k in range(8):
                    blk = b8 * 8 + k
                    s.tr(ptb[:, k * 128:(k + 1) * 128], VT.t[:, blk * 128:(blk + 1) * 128], C.ident.t[:],
                         [VT.b[0], C.ident.b[0]], [pt.b[0]])
                ptv = ptb.rearrange("p (k c) -> p k c", c=128)
                for h2 in range(2):
                    s.cp("dve", VA[h2].t[:, b8 * 8:(b8 + 1) * 8, 0:64], ptv[:, :, h2 * 64:h2 * 64 + 64],
                         [pt.b[0]], VA[h2].b)
        Qh, Kh, Vh = Q[hh], K[hh], VA[hh]
        for qt in range(16):
            q0 = qt * 512
            nkb = 4 * qt + 4
            po = psO.get()
            pend = []

            def qk(kb):
                j = kb - 4 * qt
                c0 = 128 * j if j > 0 else 0
                ps = psS.get()
                s.mm(ps.t[:, c0:512], Kh.t[0:96, kb * 128:(kb + 1) * 128], Qh.t[0:96, q0 + c0:q0 + 512],
                     True, j < 0, [Kh.b[0], Qh.b[0]], [ps.b[0]])
                if j >= 0:
                    d0 = 128 * j
                    s.mm(ps.t[:, d0:d0 + 128], C.ident.t[:], tri.t[:], False, True, [C.ident.b[0], tri.b[0]],
                         [ps.b[0]])
                return ps, c0, j

            def rest(kb, ps, c0, j):
                nonlocal pi
                p = pT[pi % NP]
                pi += 1
                s.act(p.t[:, c0:512], ps.t[:, c0:512], AF.Exp, [ps.b[0]], [p.b[0]], scale=scale)
                s.mm(po.t[:, c0:512], Vh.t[:, kb, :], p.t[:, c0:512], kb == 0, kb == nkb - 1,
                     [Vh.b[0], p.b[0]], [po.b[0]])
            LOOK = 3
            q_ = []
            for kb in range(min(LOOK, nkb)):
                q_.append((kb,) + qk(kb))
            for kb in range(nkb):
                if kb + LOOK < nkb:
                    q_.append((kb + LOOK,) + qk(kb + LOOK))
                a = q_.pop(0)
                rest(*a)
            r = rd[qt % 2]
            s.recip(r.t[64:128, :], po.t[64:128, :], [po.b[0]], [r.b[0]])
            s.tt("dve", OST.t[hh * 64:hh * 64 + 64, q0:q0 + 512], po.t[0:64, :], r.t[64:128, :], ALU.mult,
                 [po.b[0], r.b[0]], [OST.b[0]])
        if hh == 1:
            s.dma("pool", dsc["OT"][4 + h // 2], OST.t[:], reads=[OST.b[0]])
    s.pop()


def phase_D(s, C, din, dsc, L, x_src):
    s.push()
    TT = 512
    p = "l%d_" % L
    w_out = load_weight(s, C.stg, din[p + "w_out"], 1024, 1024, "w_out")
    w_xq = load_weight(s, C.stg, din[p + "w_xq"], 1024, 512, "w_xq")
    w_xo = load_weight(s, C.stg, din[p + "w_xo"], 512, 1024, "w_xo")
    w_xkv = load_weight(s, C.stg, din[p + "w_xkv"], 1024, 1024, "w_xkv")
    g_x = load_small(s, din[p + "x_norm"][:, :], [128, 8], F32, "g_x")
    g_m = load_small(s, din[p + "mem_norm"][:, :], [128, 8], F32, "g_m")
    psp = PsPool(s, range(8))
    mt = s.tile([128, 8, 256], F32, "memt")
    s.dma("sp", mt.t[:], din["memT"].rearrange("(c p) t -> p c t", p=128), writes=[mt.b[0]])
    msq = s.tile([128, 8, 256], BF16, "msq")
    s.act(msq.t[:], mt.t[:], AF.Square, [mt.b[0]], [msq.b[0]])
    mr = s.tile([128, 256], F32, "mr")
    rms_stat(s, C, 1024, msq, range(8), mr, psp, 256, lnexp=True)
    mn = s.tile([128, 8, 256], BF16, "mn")
    for c in range(8):
        s.stt(mn.t[:, c, :], mt.t[:, c, :], g_m.t[:, c:c + 1], mr.t[:], ALU.mult, ALU.mult,
              [mt.b[0], g_m.b[0], mr.b[0]], [mn.b[0]])
    Kx = s.tile([128, 4, 256], BF16, "Kx")
    for h in range(4):
        ps = psp.get()
        for i in range(8):
            s.mm(ps.t[:, 0:256], w_xkv.t[:, i, h * 128:(h + 1) * 128], mn.t[:, i, :], i == 0, i == 7,
                 [w_xkv.b[0], mn.b[0]], [ps.b[0]])
        s.cp("act", Kx.t[:, h, :], ps.t[:, 0:256], [ps.b[0]], [Kx.b[0]])
    Vx = s.tile([128, 2, 512], BF16, "Vx")
    for b in range(2):
        ps = psp.get()
        for i in range(8):
            s.mm(ps.t[:], mn.t[:, i, b * 128:(b + 1) * 128], w_xkv.t[:, i, 512:1024], i == 0, i == 7,
                 [w_xkv.b[0], mn.b[0]], [ps.b[0]])
        s.cp("act", Vx.t[:, b, :], ps.t[:], [ps.b[0]], [Vx.b[0]])
    NB = 2
    ot = [s.tile([128, 8, TT], BF16, "ot") for _ in range(NB)]
    xt = [s.tile([128, 8, TT], F32, "xt") for _ in range(NB)]
    sqs = [s.tile([128, 8, TT], BF16, "sq") for _ in range(NB)]
    xn = s.tile([128, 8, TT], BF16, "xn")
    rstd = s.tile([128, TT], F32, "rstd")
    qx = s.tile([128, 4, TT], BF16, "qx")
    pT = s.tile([128, 8, TT], BF16, "pT", nb=8)
    at = s.tile([128, 4, TT], BF16, "at", nb=4)
    rden = [s.tile([128, TT], F32, "rden") for _ in range(2)]
    xsrc = x_src.rearrange("(c p) t -> p c t", p=128)
    rdst = dsc["R"].rearrange("(c p) t -> p c t", p=128)
    osrc = dsc["OT"].rearrange("c p t -> p c t")
    scale = 128 ** -0.5
    NTL = S // TT

    def stage_a_pieces(it):
        sl = slice(it * TT, (it + 1) * TT)
        o, x = ot[it % NB], xt[it % NB]

        def loads():
            s.dma("sp", o.t[:], osrc[:, :, sl], writes=[o.b[0]])
            s.dma("sp", x.t[:], xsrc[:, :, sl], writes=[x.b[0]])

        def chunk(m):
            ps = psp.get()
            for i in range(8):
                s.mm(ps.t[:], w_out.t[:, i, m * 128:(m + 1) * 128], o.t[:, i, :], i == 0, i == 7,
                     [w_out.b[0], o.b[0]], [ps.b[0]])
            s.tt("dve", x.t[:, m, :], x.t[:, m, :], ps.t[:], ALU.add, [ps.b[0], x.b[0]], [x.b[0]])

        def fin():
            s.act(sqs[it % NB].t[:], x.t[:], AF.Square, [x.b[0]], [sqs[it % NB].b[0]])
        pcs = [loads]
        for m in range(0, 8, 2):
            pcs.append(lambda m=m: (chunk(m), chunk(m + 1)))
        pcs.append(fin)
        return pcs

    def stage_a(it):
        for p_ in stage_a_pieces(it):
            p_()

    def stage_b(it, nxt):
        def gap():
            if nxt:
                nxt.pop(0)()
        sl = slice(it * TT, (it + 1) * TT)
        x = xt[it % NB]
        gap()
        rms_stat(s, C, 1024, sqs[it % NB], range(8), rstd, psp, TT, lnexp=True)
        gap()
        for c in range(8):
            s.stt(xn.t[:, c, :], x.t[:, c, :], g_x.t[:, c:c + 1], rstd.t[:], ALU.mult, ALU.mult,
                  [x.b[0], g_x.b[0], rstd.b[0]], [xn.b[0]])
        for h in range(4):
            ps = psp.get()
            for i in range(8):
                s.mm(ps.t[:], w_xq.t[:, i, h * 128:(h + 1) * 128], xn.t[:, i, :], i == 0, i == 7,
                     [w_xq.b[0], xn.b[0]], [ps.b[0]])
            s.cp("act", qx.t[:, h, :], ps.t[:], [ps.b[0]], [qx.b[0]])
        gap()
        for h in range(4):
            for b in range(2):
                ps = psp.get()
                s.mm(ps.t[:], Kx.t[:, h, b * 128:(b + 1) * 128], qx.t[:, h, :], True, True, [Kx.b[0], qx.b[0]],
                     [ps.b[0]])
                s.act(pT.t[:, h * 2 + b, :], ps.t[:], AF.Exp, [ps.b[0]], [pT.b[h * 2 + b]], scale=scale)
        gap()
        for h in range(4):
            po = psp.get()
            pd = psp.get()
            for b in range(2):
                s.mm(po.t[:], Vx.t[:, b, h * 128:(h + 1) * 128], pT.t[:, h * 2 + b, :], b == 0, b == 1,
                     [Vx.b[0], pT.b[h * 2 + b]], [po.b[0]])
            for b in range(2):
                s.mm(pd.t[:], C.ones[1].t[:], pT.t[:, h * 2 + b, :], b == 0, b == 1,
                     [C.ones[1].b[0], pT.b[h * 2 + b]], [pd.b[0]])
            r = rden[h % 2]
            s.act(r.t[:], pd.t[:], AF.Ln, [pd.b[0]], [r.b[0]])
            s.act(r.t[:], r.t[:], AF.Exp, [r.b[0]], [r.b[0]], scale=-1.0)
            s.tt("dve", at.t[:, h, :], po.t[:], r.t[:], ALU.mult, [po.b[0], r.b[0]], [at.b[h]])
        gap()
        for m in range(8):
            ps = psp.get()
            for i in range(4):
                s.mm(ps.t[:], w_xo.t[:, i, m * 128:(m + 1) * 128], at.t[:, i, :], i == 0, i == 3,
                     [w_xo.b[0], at.b[i]], [ps.b[0]])
            s.tt("dve", x.t[:, m, :], x.t[:, m, :], ps.t[:], ALU.add, [ps.b[0], x.b[0]], [x.b[0]])
        s.dma("pool", rdst[:, :, sl], x.t[:], reads=[x.b[0]])
        while nxt:
            nxt.pop(0)()

    stage_a(0)
    for it in range(NTL):
        stage_b(it, stage_a_pieces(it + 1) if it + 1 < NTL else [])
    s.pop()


def phase_E(s, C, din, dsc, L, final, y_dst):
    s.push()
    TT = 256
    NTL = S // TT
    p = "l%d_" % L
    wg = load_weight(s, C.stg, din[p + "w_gate"], 1024, HID, "wg")
    wu = load_weight(s, C.stg, din[p + "w_up"], 1024, HID, "wu")
    g_f = load_small(s, din[p + "ffn_norm"][:, :], [128, 8], F32, "g_f")
    if final:
        g_fin = load_small(s, din["final_norm"][:, :], [128, 8], F32, "g_fin")
    xt = [s.tile([128, 8, TT], F32, "xt") for _ in range(2)]
    sq = s.tile([128, 8, TT], BF16, "sq")
    rstd = s.tile([128, TT], F32, "rstd")
    xn = s.tile([128, 8, TT], BF16, "xn")
    act = s.tile([128, 22, TT], BF16, "act", nb=22)
    sg = [s.tile([128, TT], BF16, "sg") for _ in range(2)]
    xo = [s.tile([128, 8, TT], F32, "xo") for _ in range(2)]
    if final:
        sqf = s.tile([128, 8, TT], BF16, "sqf")
        rf = s.tile([128, TT], F32, "rf")
    psG = PsPool(s, (0, 1))
    psU = PsPool(s, (2, 3))
    psD = PsPool(s, (4, 5))
    psN = PsPool(s, (6, 7))
    rsrc = dsc["R"].rearrange("(c p) t -> p c t", p=128)
    ydst = y_dst.rearrange("(c p) t -> p c t", p=128) if final else None

    def load(it):
        s.dma("sp", xt[it % 2].t[:], rsrc[:, :, it * TT:(it + 1) * TT], writes=[xt[it % 2].b[0]])

    def norm_pre(it):
        x = xt[it % 2]
        s.act(sq.t[:], x.t[:], AF.Square, [x.b[0]], [sq.b[0]])

    def norm_stat(it):
        rms_stat(s, C, 1024, sq, range(8), rstd, psN, TT)

    def norm_post(it):
        x = xt[it % 2]
        for c in range(8):
            s.stt(xn.t[:, c, :], x.t[:, c, :], g_f.t[:, c:c + 1], rstd.t[:], ALU.mult, ALU.mult,
                  [x.b[0], g_f.b[0], rstd.b[0]], [xn.b[0]])

    def fin_stat(it):
        o = xo[it % 2]
        s.act(sqf.t[:], o.t[:], AF.Square, [o.b[0]], [sqf.b[0]])
        rms_stat(s, C, 1024, sqf, range(8), rf, psN, TT)
        for c in range(8):
            s.stt(o.t[:, c, :], o.t[:, c, :], g_fin.t[:, c:c + 1], rf.t[:], ALU.mult, ALU.mult,
                  [o.b[0], g_fin.b[0], rf.b[0]], [o.b[0]])
        s.dma("pool", ydst[:, :, it * TT:(it + 1) * TT], o.t[:], reads=[o.b[0]])

    load(0)
    norm_pre(0)
    norm_stat(0)
    norm_post(0)
    wd = load_weight(s, C.stg, din[p + "w_down"], HID, 1024, "wd")
    for it in range(NTL):
        x = xt[it % 2]
        if it + 1 < NTL:
            load(it + 1)
        for j in range(22):
            pg = psG.get()
            pu = psU.get()
            for i in range(8):
                s.mm(pg.t[:, 0:TT], wg.t[:, i, j * 128:(j + 1) * 128], xn.t[:, i, :], i == 0, i == 7,
                     [wg.b[0], xn.b[0]], [pg.b[0]])
            for i in range(8):
                s.mm(pu.t[:, 0:TT], wu.t[:, i, j * 128:(j + 1) * 128], xn.t[:, i, :], i == 0, i == 7,
                     [wu.b[0], xn.b[0]], [pu.b[0]])
            g = sg[j % 2]
            s.act(g.t[:], pg.t[:, 0:TT], AF.Silu, [pg.b[0]], [g.b[0]])
            s.tt("dve", act.t[:, j, :], pu.t[:, 0:TT], g.t[:], ALU.mult, [pu.b[0], g.b[0]], [act.b[j]])
            if j == 2 and it + 1 < NTL:
                norm_pre(it + 1)
        if it + 1 < NTL:
            norm_stat(it + 1)
        if final and it > 0:
            fin_stat(it - 1)
        o = xo[it % 2]
        for m in range(8):
            pd = psD.get()
            for j in range(22):
                s.mm(pd.t[:, 0:TT], wd.t[:, j, m * 128:(m + 1) * 128], act.t[:, j, :], j == 0, j == 21,
                     [wd.b[0], act.b[j]], [pd.b[0]])
            s.tt("dve", o.t[:, m, :], pd.t[:, 0:TT], x.t[:, m, :], ALU.add, [pd.b[0], x.b[0]], [o.b[0]])
            if m == 0 and it + 1 < NTL:
                norm_post(it + 1)
        if not final:
            s.dma("pool", rsrc[:, :, it * TT:(it + 1) * TT], o.t[:], reads=[o.b[0]])
    if final:
        fin_stat(NTL - 1)
    s.pop()


IN_SHAPES = {
    "xT": ([1024, S], F32), "memT": ([1024, 256], F32), "pos": ([1, S], I32),
    "perm64": ([128, 128], F32), "perm96": ([128, 128], F32), "invf64": ([128, 1], F32), "invf96": ([128, 1], F32),
    "l0_mix_norm": ([128, 8], F32), "l0_w_in": ([1024, 1632], F32), "l0_sinks": ([128, 8], F32),
    "l0_q_norm": ([128, 3], F32), "l0_w_uq": ([384, 768], F32), "l0_kv_norm": ([128, 2], F32),
    "l0_w_ukv": ([256, 1024], F32), "l0_w_out": ([1024, 1024], F32),
    "l1_mix_norm": ([128, 8], F32), "l1_w_qkv": ([1024, 3072], F32), "l1_w_out": ([1024, 1024], F32),
    "final_norm": ([128, 8], F32),
}
for _L in (0, 1):
    _p = "l%d_" % _L
    IN_SHAPES.update({
        _p + "x_norm": ([128, 8], F32), _p + "mem_norm": ([128, 8], F32), _p + "w_xq": ([1024, 512], F32),
        _p + "w_xkv": ([1024, 1024], F32), _p + "w_xo": ([512, 1024], F32), _p + "ffn_norm": ([128, 8], F32),
        _p + "w_gate": ([1024, HID], F32), _p + "w_up": ([1024, HID], F32), _p + "w_down": ([HID, 1024], F32),
    })

SCRATCH = {
    "R": ([1024, S], F32), "QA": ([4, 128, S], BF16), "KA": ([2, 128, S], BF16), "VA": ([128, S], BF16),
    "QM": ([8, 96, S], BF16), "KM": ([8, 96, S], BF16), "VM": ([4, 128, S], BF16), "OT": ([8, 128, S], BF16),
    "Q1": ([8, 128, S], BF16), "K1": ([8, 128, S], BF16), "V1": ([8, 128, S], BF16),
}

ALL_PHASES = ("A0", "B0", "C0", "D0", "E0", "A1", "B1", "D1", "E1")


def build(phases=ALL_PHASES, debug=()):
    nc = bass.Bass("TRN2", target_bir_lowering=False)
    din = {k: nc.dram_tensor(k, sh, dt, kind="ExternalInput").ap() for k, (sh, dt) in IN_SHAPES.items()}
    dsc = {}
    for k, (sh, dt) in SCRATCH.items():
        kind = "ExternalOutput" if k in debug else "Internal"
        dsc[k] = nc.dram_tensor("sc_" + k, sh, dt, kind=kind).ap()
    yT = nc.dram_tensor("yT", [1024, S], F32, kind="ExternalOutput").ap()
    dsc["_yT"] = yT
    with ExitStack() as es:
        s = Sched(nc, es)
        C = Ctx()
        consts_common(s, C, din)
        C.eps = s.tile([128, 1], F32, "eps")
        s.memset("dve", C.eps.t[:], EPS, [C.eps.b[0]])
        s.barrier()
        for ph in phases:
            if ph == "A0":
                phase_A0(s, C, din, dsc, din["xT"])
            elif ph == "B0":
                phase_band(s, C, din, dsc, 0)
            elif ph == "C0":
                phase_mla(s, C, din, dsc)
            elif ph == "D0":
                phase_D(s, C, din, dsc, 0, din["xT"])
            elif ph == "E0":
                phase_E(s, C, din, dsc, 0, False, None)
            elif ph == "A1":
                phase_A1(s, C, din, dsc, dsc["R"])
            elif ph == "B1":
                phase_band(s, C, din, dsc, 1)
            elif ph == "D1":
                phase_D(s, C, din, dsc, 1, dsc["R"])
            elif ph == "E1":
                phase_E(s, C, din, dsc, 1, True, yT)
            s.barrier()
        s.emit()
    return nc


def _colvec(g):
    g = np.asarray(g, np.float32)
    return np.ascontiguousarray(g.reshape(-1, 128).T)


def _consts():
    c = {}
    p64 = np.zeros((128, 128), np.float32)
    for m in range(128):
        j = m % 64
        if j < 32:
            p64[m + 32, m] = -1.0
        else:
            p64[m - 32, m] = 1.0
    c["perm64"] = p64
    p96 = np.zeros((128, 128), np.float32)
    for j in range(32):
        m = 64 + j
        if j < 16:
            p96[m + 16, m] = -1.0
        else:
            p96[m - 16, m] = 1.0
    c["perm96"] = p96
    f64 = (np.float32(10000.0) ** (-np.arange(0, 64, 2, dtype=np.float32) / np.float32(64))).astype(np.float32)
    f32 = (np.float32(10000.0) ** (-np.arange(0, 32, 2, dtype=np.float32) / np.float32(32))).astype(np.float32)
    c["invf64"] = np.ascontiguousarray(f64[np.arange(128) % 32][:, None])
    v = np.zeros((128, 1), np.float32)
    v[64:96, 0] = f32[np.arange(32) % 16]
    c["invf96"] = v
    return c


def make_in_maps(inputs):
    shared = _consts()
    w_in = np.asarray(inputs["l0_w_in"], np.float32)
    qa, ka, va, cqw, ckvw, krw = np.split(w_in, [512, 640, 768, 1152, 1408], axis=1)
    ka_dup = np.concatenate([ka[:, 0:64], ka[:, 0:64], ka[:, 64:128], ka[:, 64:128]], axis=1)
    krp = np.concatenate([np.zeros((1024, 64), np.float32), krw], axis=1)
    shared["l0_w_in"] = np.ascontiguousarray(np.concatenate([qa, ka_dup, va, cqw, ckvw, krp], axis=1))
    shared["l0_w_uq"] = np.ascontiguousarray(inputs["l0_w_uq"], np.float32)
    wukv = np.asarray(inputs["l0_w_ukv"], np.float32).reshape(256, 8, 128)
    shared["l0_w_ukv"] = np.ascontiguousarray(
        np.concatenate([wukv[:, :, 0:64].reshape(256, 512), wukv[:, :, 64:128].reshape(256, 512)], axis=1))
    shared["l0_sinks"] = np.ascontiguousarray(np.broadcast_to(np.asarray(inputs["l0_sinks"], np.float32)[None, :],
                                                              (128, 8)))
    for k in ("l0_mix_norm", "l0_q_norm", "l0_kv_norm", "l1_mix_norm", "final_norm", "l0_x_norm", "l0_mem_norm",
              "l0_ffn_norm", "l1_x_norm", "l1_mem_norm", "l1_ffn_norm"):
        shared[k] = _colvec(inputs[k])
    for k in ("l0_w_out", "l1_w_qkv", "l1_w_out", "l0_w_xq", "l0_w_xkv", "l0_w_xo", "l0_w_gate", "l0_w_up",
              "l0_w_down", "l1_w_xq", "l1_w_xkv", "l1_w_xo", "l1_w_gate", "l1_w_up", "l1_w_down"):
        shared[k] = np.ascontiguousarray(inputs[k], np.float32)
    maps = []
    x = np.asarray(inputs["x"], np.float32)
    mem = np.asarray(inputs["mem"], np.float32)
    pos = np.asarray(inputs["positions"], np.int32)
    for b in range(NCORE):
        m = dict(shared)
        m["xT"] = np.ascontiguousarray(x[b].T)
        m["memT"] = np.ascontiguousarray(mem[b].T)
        m["pos"] = np.ascontiguousarray(pos[b][None, :])
        maps.append(m)
    return maps


_NC_CACHE = {}


def kernel(**inputs):
    if "nc" not in _NC_CACHE:
        _NC_CACHE["nc"] = build()
    nc = _NC_CACHE["nc"]
    maps = make_in_maps(inputs)
    res = run_bass_kernel_spmd(nc, maps, core_ids=list(range(NCORE)))
    out = np.empty((NCORE, S, D), np.float32)
    for b in range(NCORE):
        out[b] = np.asarray(res.results[b]["yT"], np.float32).T
    return out
```
